# Optimizing a Trainium2 kernel written in Bass

```python
import math
import jax, jax.numpy as jnp
from jax import lax
import numpy as np

D_MODEL = 1024
BATCH = 4
SEQ = 4096
DEPTH = 1
DEC_BATCH = 32
DEC_SEQ = 4
PAST_LEN = 8192
PAGE_SIZE = 128

A_HEADS = 8
A_HEAD_DIM = 64
A_WIDTH = A_HEADS * A_HEAD_DIM
B_HEADS = 4
B_KEY_DIM = 128
B_VAL_DIM = 128
B_KEY_WIDTH = B_HEADS * B_KEY_DIM
B_WIDTH = B_HEADS * B_VAL_DIM
MIX_WIDTH = A_WIDTH + B_WIDTH
PROJ_SPLITS = (A_WIDTH, A_WIDTH, A_WIDTH, B_KEY_WIDTH, B_KEY_WIDTH, B_WIDTH, B_WIDTH)
IN_WIDTH = 3 * A_WIDTH + 2 * B_KEY_WIDTH + 2 * B_WIDTH
MOBA_BLOCK = 256
MOBA_TOPK = 3
Q_CHUNK = 64
HG_CHUNK = 16
N_BUCKETS = 32
REL_MAX_DIST = 4096
D_FF = 2816
NORM_EPS = 1e-6
NEG_INF = -1e30

kernel_name = 'hymba_moba_hgrn2_macaron_step'


def rms_norm(x, gain):
    xf = x.astype(jnp.float32)
    y = xf * lax.rsqrt(jnp.mean(xf * xf, axis=-1, keepdims=True) + NORM_EPS)
    return (y * gain.astype(jnp.float32)).astype(x.dtype)


def swiglu(x, w_gate, w_up, w_down):
    return (jax.nn.silu(x @ w_gate) * (x @ w_up)) @ w_down


def rel_bucket(dist):
    max_exact = N_BUCKETS // 2
    n = jnp.maximum(dist, 0)
    nf = jnp.maximum(n, 1).astype(jnp.float32)
    large = max_exact + (jnp.log(nf / max_exact) / math.log(REL_MAX_DIST / max_exact)
                         * (N_BUCKETS - max_exact)).astype(jnp.int32)
    large = jnp.minimum(large, N_BUCKETS - 1)
    return jnp.where(n < max_exact, n, large)


def project(h, w_in, q_gain, k_gain, lower_bound):
    b, t, _ = h.shape
    z = h @ w_in
    offsets = [int(o) for o in np.cumsum(PROJ_SPLITS)[:-1]]
    qa, ka, va, qb, fb, ib, gb = jnp.split(z, offsets, axis=-1)
    qa = rms_norm(qa.reshape(b, t, A_HEADS, A_HEAD_DIM), q_gain)
    ka = rms_norm(ka.reshape(b, t, A_HEADS, A_HEAD_DIM), k_gain)
    va = va.reshape(b, t, A_HEADS, A_HEAD_DIM)
    lb = lower_bound.astype(jnp.float32)
    forget = lb + (1.0 - lb) * jax.nn.sigmoid(fb.astype(jnp.float32))
    log_f = jnp.log(forget).reshape(b, t, B_HEADS, B_KEY_DIM)
    kb = (1.0 - forget).reshape(b, t, B_HEADS, B_KEY_DIM)
    qb = jax.nn.silu(qb).reshape(b, t, B_HEADS, B_KEY_DIM)
    ib = ib.reshape(b, t, B_HEADS, B_VAL_DIM)
    return qa, ka, va, qb, kb, ib, log_f, gb


def hgrn2_chunked(q, k, v, log_f, s0):
    b, t, h, _ = q.shape
    c = min(HG_CHUNK, t)
    n = -(-t // c)
    pad = n * c - t

    def prep(a):
        a = jnp.pad(a.astype(jnp.float32), ((0, 0), (0, pad), (0, 0), (0, 0)))
        return a.reshape(b, n, c, h, a.shape[-1]).transpose(1, 0, 3, 2, 4)

    qc, kc, vc, gc = prep(q), prep(k), prep(v), prep(log_f)
    causal = jnp.tril(jnp.ones((c, c), dtype=bool))[:, :, None]

    def step(s, inp):
        qq, kk, vv, gg = inp
        cum = jnp.cumsum(gg, axis=2)
        inter = jnp.einsum('bhtk,bhkv->bhtv', qq * jnp.exp(cum), s)
        diff = cum[:, :, :, None, :] - cum[:, :, None, :, :]
        decay = jnp.exp(jnp.where(causal, diff, -jnp.inf))
        scores = jnp.einsum('bhtk,bhtsk,bhsk->bhts', qq, decay, kk)
        intra = jnp.einsum('bhts,bhsv->bhtv', scores, vv)
        last = cum[:, :, -1:, :]
        s_new = jnp.exp(last[:, :, 0, :])[..., None] * s + jnp.einsum(
            'bhsk,bhsv->bhkv', kk * jnp.exp(last - cum), vv)
        return s_new, inter + intra

    s_fin, out = lax.scan(step, s0.astype(jnp.float32), (qc, kc, vc, gc))
    out = out.transpose(1, 0, 3, 2, 4).reshape(b, n * c, h, -1)[:, :t]
    return out, s_fin


def moba_attend(q, q_pos, k_sel, v_sel, kpos_sel, sel_valid, k_own, v_own, kpos_own, rel_table):
    scale = A_HEAD_DIM ** -0.5
    table_t = rel_table.astype(jnp.float32).T
    own_dist = q_pos[:, None] - kpos_own[None, :]
    own_logits = jnp.einsum('bhtd,bhnd->bhtn', q, k_own, preferred_element_type=jnp.float32) * scale
    own_logits = own_logits + jnp.take(table_t, rel_bucket(own_dist), axis=1)[None]
    own_logits = jnp.where((own_dist >= 0)[None, None], own_logits, NEG_INF)
    if k_sel is None:
        p = jax.nn.softmax(own_logits, axis=-1)
        return jnp.einsum('bhtn,bhnd->bhtd', p.astype(v_own.dtype), v_own,
                          preferred_element_type=jnp.float32)
    heads = jnp.arange(q.shape[1])[None, :, None, None, None]
    sel_dist = q_pos[None, None, :, None, None] - kpos_sel
    sel_logits = jnp.einsum('bhtd,bhtjld->bhtjl', q, k_sel, preferred_element_type=jnp.float32) * scale
    sel_logits = sel_logits + table_t[heads, rel_bucket(sel_dist)]
    sel_logits = jnp.where(sel_valid, sel_logits, NEG_INF)
    b, h, t, j, L = sel_logits.shape
    logits = jnp.concatenate([sel_logits.reshape(b, h, t, j * L), own_logits], axis=-1)
    p = jax.nn.softmax(logits, axis=-1)
    p_sel = p[..., :j * L].reshape(b, h, t, j, L)
    p_own = p[..., j * L:]
    return (jnp.einsum('bhtjl,bhtjld->bhtd', p_sel.astype(v_sel.dtype), v_sel,
                       preferred_element_type=jnp.float32)
            + jnp.einsum('bhtn,bhnd->bhtd', p_own.astype(v_own.dtype), v_own,
                         preferred_element_type=jnp.float32))


def moba_prompt(q, k, v, rel_table):
    b, s, h, d = q.shape
    nb = -(-s // MOBA_BLOCK)
    pad = nb * MOBA_BLOCK - s
    qh = q.transpose(0, 2, 1, 3)
    kh = jnp.pad(k.transpose(0, 2, 1, 3), ((0, 0), (0, 0), (0, pad), (0, 0)))
    vh = jnp.pad(v.transpose(0, 2, 1, 3), ((0, 0), (0, 0), (0, pad), (0, 0)))
    k_blocks = kh.reshape(b, h, nb, MOBA_BLOCK, d)
    v_blocks = vh.reshape(b, h, nb, MOBA_BLOCK, d)
    n_full = jnp.arange(s) // MOBA_BLOCK
    n_sel = min(MOBA_TOPK, nb - 1)
    offs = jnp.arange(MOBA_BLOCK)
    bi = jnp.arange(b)[:, None, None, None]
    hi = jnp.arange(h)[None, :, None, None]
    if n_sel > 0:
        k_mean = jnp.mean(k_blocks.astype(jnp.float32), axis=3)
        gate = jnp.einsum('bhsd,bhnd->bhsn', qh.astype(jnp.float32), k_mean)
        gate = jnp.where(jnp.arange(nb)[None, :] < n_full[:, None], gate, -jnp.inf)
        _, sel_idx = lax.top_k(gate, n_sel)
        sel_valid = jnp.arange(n_sel)[None, :] < n_full[:, None]

    def chunk(ci):
        c0 = ci * Q_CHUNK
        qc = lax.dynamic_slice_in_dim(qh, c0, Q_CHUNK, axis=2)
        q_pos = c0 + jnp.arange(Q_CHUNK)
        own_start = (c0 // MOBA_BLOCK) * MOBA_BLOCK
        k_own = lax.dynamic_slice_in_dim(kh, own_start, MOBA_BLOCK, axis=2)
        v_own = lax.dynamic_slice_in_dim(vh, own_start, MOBA_BLOCK, axis=2)
        kpos_own = own_start + offs
        if n_sel == 0:
            return moba_attend(qc, q_pos, None, None, None, None, k_own, v_own, kpos_own, rel_table)
        idx = lax.dynamic_slice_in_dim(sel_idx, c0, Q_CHUNK, axis=2)
        valid = lax.dynamic_slice_in_dim(sel_valid, c0, Q_CHUNK, axis=0)[None, None, :, :, None]
        k_sel = k_blocks[bi, hi, idx]
        v_sel = v_blocks[bi, hi, idx]
        kpos_sel = idx[..., None] * MOBA_BLOCK + offs
        return moba_attend(qc, q_pos, k_sel, v_sel, kpos_sel, valid, k_own, v_own, kpos_own, rel_table)

    out = lax.map(chunk, jnp.arange(s // Q_CHUNK))
    return out.transpose(1, 0, 3, 2, 4).reshape(b, s, h, d)


def moba_sample(q, k_new, v_new, cache_k, cache_v, page_table, rel_table):
    db, t, h, d = q.shape
    k_past = cache_k[page_table].reshape(db, -1, h, d)
    v_past = cache_v[page_table].reshape(db, -1, h, d)
    past = k_past.shape[1]
    n_full = past // MOBA_BLOCK
    own_start = n_full * MOBA_BLOCK
    q_pos = past + jnp.arange(t)
    qh = q.transpose(0, 2, 1, 3)
    k_own = jnp.concatenate([k_past[:, own_start:], k_new], axis=1).transpose(0, 2, 1, 3)
    v_own = jnp.concatenate([v_past[:, own_start:], v_new], axis=1).transpose(0, 2, 1, 3)
    kpos_own = jnp.concatenate([jnp.arange(own_start, past), q_pos])
    n_sel = min(MOBA_TOPK, n_full)
    if n_sel == 0:
        out = moba_attend(qh, q_pos, None, None, None, None, k_own, v_own, kpos_own, rel_table)
        return out.transpose(0, 2, 1, 3)
    k_blocks = k_past[:, :own_start].reshape(db, n_full, MOBA_BLOCK, h, d)
    v_blocks = v_past[:, :own_start].reshape(db, n_full, MOBA_BLOCK, h, d)
    k_mean = jnp.mean(k_blocks.astype(jnp.float32), axis=2)
    gate = jnp.einsum('bhtd,bnhd->bhtn', qh.astype(jnp.float32), k_mean)
    _, idx = lax.top_k(gate, n_sel)
    bi = jnp.arange(db)[:, None, None, None]
    hi = jnp.arange(h)[None, :, None, None]
    k_sel = k_blocks[bi, idx, :, hi]
    v_sel = v_blocks[bi, idx, :, hi]
    kpos_sel = idx[..., None] * MOBA_BLOCK + jnp.arange(MOBA_BLOCK)
    valid = jnp.ones((1, 1, 1, 1, 1), dtype=bool)
    out = moba_attend(qh, q_pos, k_sel, v_sel, kpos_sel, valid, k_own, v_own, kpos_own, rel_table)
    return out.transpose(0, 2, 1, 3)


def merge_groups(attn, rec, gate, out_gain, w_out):
    b, t = gate.shape[:2]
    rec = rms_norm(rec, out_gain.reshape(B_HEADS, B_VAL_DIM)).reshape(b, t, B_WIDTH).astype(gate.dtype)
    rec = rec * jax.nn.silu(gate)
    mixed = jnp.concatenate([attn.reshape(b, t, A_WIDTH).astype(gate.dtype), rec], axis=-1)
    return mixed @ w_out


def setup_inputs(seed: int = 0) -> dict:
    key = jax.random.key(seed)
    ks = jax.random.split(key, 24)
    n_pages = PAST_LEN // PAGE_SIZE
    n_used = DEC_BATCH * n_pages
    n_phys = n_used + n_used // 4

    def nrm(k, shape, scale):
        return scale * jax.random.normal(k, shape, jnp.float32)

    def gain(k, shape):
        return 1.0 + 0.02 * jax.random.normal(k, shape, jnp.float32)

    page_table = jax.random.permutation(ks[5], n_phys)[:n_used].reshape(DEC_BATCH, n_pages).astype(jnp.int32)
    return {
        'x_prompt': nrm(ks[0], (BATCH, SEQ, D_MODEL), 1.0),
        'x_sample': nrm(ks[1], (DEC_BATCH, DEC_SEQ, D_MODEL), 1.0),
        'cache_k': nrm(ks[2], (DEPTH, n_phys, PAGE_SIZE, A_HEADS, A_HEAD_DIM), 1.0),
        'cache_v': nrm(ks[3], (DEPTH, n_phys, PAGE_SIZE, A_HEADS, A_HEAD_DIM), 1.0),
        'state_hgrn': nrm(ks[4], (DEPTH, DEC_BATCH, B_HEADS, B_KEY_DIM, B_VAL_DIM), 0.5),
        'page_table': page_table,
        'rel_bias_table': nrm(ks[6], (N_BUCKETS, A_HEADS), 0.5),
        'lb_logits': nrm(ks[7], (DEPTH + 1, B_KEY_WIDTH), 1.0),
        'ffn1_norm': gain(ks[8], (DEPTH, D_MODEL)),
        'ffn1_gate': nrm(ks[9], (DEPTH, D_MODEL, D_FF), D_MODEL ** -0.5),
        'ffn1_up': nrm(ks[10], (DEPTH, D_MODEL, D_FF), D_MODEL ** -0.5),
        'ffn1_down': nrm(ks[11], (DEPTH, D_FF, D_MODEL), D_FF ** -0.5),
        'mix_norm': gain(ks[12], (DEPTH, D_MODEL)),
        'w_in': nrm(ks[13], (DEPTH, D_MODEL, IN_WIDTH), D_MODEL ** -0.5),
        'q_norm': gain(ks[14], (DEPTH, A_HEAD_DIM)),
        'k_norm': gain(ks[15], (DEPTH, A_HEAD_DIM)),
        'hgrn_out_norm': gain(ks[16], (DEPTH, B_WIDTH)),
        'w_out': nrm(ks[17], (DEPTH, MIX_WIDTH, D_MODEL), MIX_WIDTH ** -0.5),
        'ffn2_norm': gain(ks[18], (DEPTH, D_MODEL)),
        'ffn2_gate': nrm(ks[19], (DEPTH, D_MODEL, D_FF), D_MODEL ** -0.5),
        'ffn2_up': nrm(ks[20], (DEPTH, D_MODEL, D_FF), D_MODEL ** -0.5),
        'ffn2_down': nrm(ks[21], (DEPTH, D_FF, D_MODEL), D_FF ** -0.5),
    }


def reference(x_prompt, x_sample, cache_k, cache_v, state_hgrn, page_table, rel_bias_table,
              lb_logits, ffn1_norm, ffn1_gate, ffn1_up, ffn1_down, mix_norm, w_in, q_norm,
              k_norm, hgrn_out_norm, w_out, ffn2_norm, ffn2_gate, ffn2_up, ffn2_down):
    lower_bounds = jnp.cumsum(jax.nn.softmax(lb_logits.astype(jnp.float32), axis=0), axis=0)
    yp, ys = x_prompt, x_sample
    kp_rows, vp_rows, sp_rows, ks_rows, vs_rows, ss_rows = [], [], [], [], [], []
    for l in range(DEPTH):
        yp = yp + 0.5 * swiglu(rms_norm(yp, ffn1_norm[l]), ffn1_gate[l], ffn1_up[l], ffn1_down[l])
        ys = ys + 0.5 * swiglu(rms_norm(ys, ffn1_norm[l]), ffn1_gate[l], ffn1_up[l], ffn1_down[l])

        qa, ka, va, qb, kb, ib, lf, gb = project(rms_norm(yp, mix_norm[l]), w_in[l], q_norm[l],
                                                 k_norm[l], lower_bounds[l])
        attn = moba_prompt(qa, ka, va, rel_bias_table)
        s0 = jnp.zeros((yp.shape[0], B_HEADS, B_KEY_DIM, B_VAL_DIM), jnp.float32)
        rec, s_fin = hgrn2_chunked(qb, kb, ib, lf, s0)
        yp = yp + merge_groups(attn, rec, gb, hgrn_out_norm[l], w_out[l])
        kp_rows.append(ka)
        vp_rows.append(va)
        sp_rows.append(s_fin.astype(state_hgrn.dtype))

        qa, ka, va, qb, kb, ib, lf, gb = project(rms_norm(ys, mix_norm[l]), w_in[l], q_norm[l],
                                                 k_norm[l], lower_bounds[l])
        attn = moba_sample(qa, ka, va, cache_k[l], cache_v[l], page_table, rel_bias_table)
        rec, s_new = hgrn2_chunked(qb, kb, ib, lf, state_hgrn[l])
        ys = ys + merge_groups(attn, rec, gb, hgrn_out_norm[l], w_out[l])
        ks_rows.append(ka)
        vs_rows.append(va)
        ss_rows.append(s_new.astype(state_hgrn.dtype))

        yp = yp + 0.5 * swiglu(rms_norm(yp, ffn2_norm[l]), ffn2_gate[l], ffn2_up[l], ffn2_down[l])
        ys = ys + 0.5 * swiglu(rms_norm(ys, ffn2_norm[l]), ffn2_gate[l], ffn2_up[l], ffn2_down[l])

    k_prompt = jnp.stack(kp_rows)
    v_prompt = jnp.stack(vp_rows)
    state_prompt = jnp.stack(sp_rows)
    k_sample = jnp.stack(ks_rows)
    v_sample = jnp.stack(vs_rows)
    state_sample = jnp.stack(ss_rows)
    return (yp, ys, k_prompt, v_prompt, state_prompt, k_sample, v_sample, state_sample)
```

```python
import numpy as np
import ml_dtypes
from contextlib import ExitStack
import concourse.bass as bass
import concourse.mybir as mybir
from concourse.bass_utils import run_bass_kernel_spmd

F32 = mybir.dt.float32
BF16 = mybir.dt.bfloat16
I32 = mybir.dt.int32
ALU = mybir.AluOpType
AF = mybir.ActivationFunctionType
AX = mybir.AxisListType

D = 1024
DFF = 2816
NFC = DFF // 128
INW = 3584
NPRE = 2048
NOWN = 2048
NSAMP = 16
R_S0 = NPRE + NOWN
NROWS = R_S0 + 128
EPS = 1e-6
NEGB = -30000.0
ENGS = ["sp", "act", "dve", "pool", "pe"]
HC_RM16 = 0
HC_RM128 = 512
HC_M16 = 1024
HC_CM16 = 1152
HC_RM4 = 1160
HC_M4 = 1176
HC_CM4 = 1192
HC_ONES = 1196
HC_J = 1324
HC_N = 1452
TV_NEG = 512
TV_N = 4608
OHR_N = 8704


class NS:
    def __init__(self, **kw):
        self.__dict__.update(kw)


class Op:
    __slots__ = ("eng", "fn", "deps", "mark", "dma", "dma_val", "cnt", "idx")


class Sched:
    def __init__(self):
        self.ops = {e: [] for e in ENGS}
        self.lastw = {}
        self.readers = {}
        self.dma_cnt = {}
        self.all_dma = []

    def _dep(self, op, prev, kind):
        if prev is None or prev is op:
            return
        if prev.dma is None and prev.eng == op.eng:
            if op.eng == "pe":
                return
            if kind != "RAW":
                return
        op.deps.add(prev)

    def add(self, eng, fn, reads=(), writes=(), dma=None):
        op = Op()
        op.eng, op.fn, op.deps, op.mark, op.dma = eng, fn, set(), False, dma
        op.dma_val = 0
        op.cnt = 0
        if dma is not None:
            self.dma_cnt[dma] = self.dma_cnt.get(dma, 0) + 16
            op.dma_val = self.dma_cnt[dma]
            self.all_dma.append(op)
        for k in reads:
            self._dep(op, self.lastw.get(k), "RAW")
        for k in writes:
            self._dep(op, self.lastw.get(k), "WAW")
            for r in self.readers.get(k, ()):
                self._dep(op, r, "WAR")
        for k in reads:
            lst = self.readers.setdefault(k, [])
            if op.dma is None:
                lst[:] = [r for r in lst if not (r.dma is None and r.eng == op.eng)]
            lst.append(op)
        for k in writes:
            self.lastw[k] = op
            self.readers[k] = []
        op.idx = len(self.ops[eng])
        self.ops[eng].append(op)
        return op

    def barrier(self):
        lasts = []
        for e in ENGS:
            if self.ops[e]:
                lasts.append(self.ops[e][-1])
        dmas = list(self.all_dma)
        for e in ENGS:
            op = self.add(e, None)
            for l in lasts:
                if l.dma is None and l.eng != e:
                    op.deps.add(l)
            seen = {}
            for d_ in dmas:
                seen[d_.dma] = d_
            for d_ in seen.values():
                op.deps.add(d_)
        self.lastw.clear()
        self.readers.clear()

    def emit(self, nc, es):
        for e in ENGS:
            for op in self.ops[e]:
                for d_ in op.deps:
                    if d_.dma is None:
                        d_.mark = True
        for e in ENGS:
            c = 0
            for op in self.ops[e]:
                if op.dma is None:
                    if op.mark and op.fn is not None:
                        c += 1
                    op.cnt = c
        esem = {e: es.enter_context(nc.semaphore("e_" + e)) for e in ENGS}
        dsem = {k: es.enter_context(nc.semaphore("d_" + k)) for k in self.dma_cnt}
        block = es.enter_context(nc.Block())
        sched = self

        def run(e, eng):
            waited = {}
            for op in sched.ops[e]:
                for d_ in op.deps:
                    if d_.dma is not None:
                        key, val = ("d", d_.dma), d_.dma_val
                        sem = dsem[d_.dma]
                    else:
                        key, val = ("e", d_.eng), d_.cnt
                        sem = esem[d_.eng]
                    if val <= 0 or waited.get(key, 0) >= val:
                        continue
                    waited[key] = val
                    eng.wait_ge(sem, val)
                if op.fn is None:
                    continue
                ins = op.fn(eng)
                if op.dma is not None:
                    ins.then_inc(dsem[op.dma], 16)
                elif op.mark:
                    ins.then_inc(esem[e], 1)
            if e == "sp":
                for k, v in sched.dma_cnt.items():
                    eng.wait_ge(dsem[k], v)

        @block.sync
        def _(eng):
            run("sp", eng)

        @block.scalar
        def _(eng):
            run("act", eng)

        @block.vector
        def _(eng):
            run("dve", eng)

        @block.gpsimd
        def _(eng):
            run("pool", eng)

        @block.tensor
        def _(eng):
            run("pe", eng)


def build_program(stop_after=None, dbg=False, with_cache=True):
    nc = bass.Bass("TRN2", target_bir_lowering=False)
    es = ExitStack()
    S = Sched()

    def din(name, shape, dt=F32):
        return nc.dram_tensor(name, list(shape), dt, kind="ExternalInput").ap()

    def dout(name, shape, dt=F32):
        return nc.dram_tensor(name, list(shape), dt, kind="ExternalOutput").ap()

    def dscr(name, shape, dt=F32):
        if dbg:
            return nc.dram_tensor(name, list(shape), dt, kind="ExternalOutput").ap()
        return nc.dram_tensor(name, list(shape), dt).ap()

    def sb(name, shape, dt=F32):
        return es.enter_context(nc.sbuf_tensor("s_" + name, list(shape), dt))

    def ps(name, shape, dt=F32):
        return es.enter_context(nc.psum_tensor(name, list(shape), dt))

    xin = din("xin", [NROWS, D])
    w1g = din("ffn1_gate", [D, DFF]); w1u = din("ffn1_up", [D, DFF]); w1d = din("ffn1_down", [DFF, D])
    w2g = din("ffn2_gate", [D, DFF]); w2u = din("ffn2_up", [D, DFF]); w2d = din("ffn2_down", [DFF, D])
    w_in = din("w_in", [D, INW]); w_out = din("w_out", [D, D])
    gains = din("gains", [3, D])
    qkg = din("qkg", [2, 64])
    lbl = din("lb_logits", [128, 2, 4])
    ident_in = din("ident", [128, 128])
    gm_in = din("gmask", [NOWN, 3, 16])

    hconst_in = din("hconst", [128, HC_N])
    ogain_in = din("ogain", [128, 4])
    blkoh_in = din("blkoh", [16, 4096], BF16)
    oht_in = din("oht", [33, TV_N])
    relb_in = din("relb", [32, 8])
    st_own = dout("st_own", [4, 128, 128])
    mixT_s = dscr("mixT_s", [8, 128, NOWN + 128], BF16)
    tvec_s = dscr("tvec_s", [8, TV_N], BF16)

    st_in = din("st_in", [4, 4, 128, 128])
    st_smp = dout("st_smp", [4, 4, 128, 128])
    if with_cache:
        cache_k = din("cache_k", [2560 * 128, 512])
        cache_v = din("cache_v", [2560 * 128, 512])
        ptab_in = din("ptab", [1, 256], I32)
        ohr_in = din("ohr", [33, OHR_N], BF16)
        smc_in = din("smc", [128, 513])

    y_own = dout("y_own", [NOWN, D])
    y_smp = dout("y_smp", [NSAMP, D])
    k_own = dout("k_own", [NOWN, 512]); v_own = dout("v_own", [NOWN, 512])
    k_smp = dout("k_smp", [NSAMP, 512]); v_smp = dout("v_smp", [NSAMP, 512])

    x1s = dscr("x1s", [NOWN + 128, D])
    kT_s = dscr("kT_s", [8, 64, 4096 + 128], BF16)
    qT_s = dscr("qT_s", [8, 80, NOWN + 128], BF16)
    v_s = dscr("v_s", [4096 + 128, 512], BF16)
    fg_s = dscr("fg_s", [4, 128, 4096 + 128])
    qb_s = dscr("qb_s", [4, 128, NOWN + 128])
    gb_s = dscr("gb_s", [4, 128, NOWN + 128])
    ib_s = dscr("ib_s", [4096 + 128, 512], BF16)

    ident = sb("ident", [128, 128])
    identb = sb("identb", [128, 128], BF16)
    g1 = sb("g1", [128, D]); gmx = sb("gmx", [128, D]); g2 = sb("g2", [128, D])
    gq = sb("gq", [128, 8, 64]); gk = sb("gk", [128, 8, 64])
    lbt = sb("lbt", [128, 2, 4])
    lbv = sb("lbv", [128, 4]); oml = sb("oml", [128, 4])
    kmean = sb("kmean", [64, 8, 16]); kmeanb = sb("kmeanb", [64, 8, 16], BF16)

    rr = {"ev": 0}

    def evac(out, in_, reads, writes, eng=None):
        if eng is None:
            eng = "act" if rr["ev"] % 2 == 0 else "dve"
            rr["ev"] += 1
        if eng == "act":
            S.add("act", lambda e: e.activation(out=out, in_=in_, func=AF.Copy), reads, writes)
        else:
            S.add(eng, lambda e: e.tensor_copy(out=out, in_=in_), reads, writes)

    def dma(q, out, in_, reads, writes, key):
        S.add(q, lambda e: e.dma_start(out=out, in_=in_), reads, writes, dma=key)

    dma("sp", ident[:], ident_in, [], ["ident"], "c0")
    S.add("dve", lambda e: e.tensor_copy(out=identb[:], in_=ident[:]), ["ident"], ["identb"])
    for i, (t, nm) in enumerate([(g1, "g1"), (gmx, "gmx"), (g2, "g2")]):
        dma("sp", t[:], gains[i:i + 1, :].partition_broadcast(128), [], [nm], "c0")
    for i, (t, nm) in enumerate([(gq, "gq"), (gk, "gk")]):
        for h in range(8):
            dma("sp", t[:, h, :], qkg[i:i + 1, :].partition_broadcast(128), [], [nm], "c0")
    dma("sp", lbt[:], lbl, [], ["lbt"], "c0")
    S.add("dve", lambda e: e.tensor_tensor(out=lbv[:], in0=lbt[:, 0, :], in1=lbt[:, 1, :], op=ALU.subtract),
          ["lbt"], ["lbv"])
    S.add("act", lambda e: e.activation(out=lbv[:], in_=lbv[:], func=AF.Sigmoid), ["lbv"], ["lbv"])
    S.add("dve", lambda e: e.tensor_scalar(out=oml[:], in0=lbv[:], scalar1=-1.0, scalar2=1.0,
                                           op0=ALU.mult, op1=ALU.add), ["lbv"], ["oml"])
    S.add("dve", lambda e: e.memset(kmean[:], 0.0), [], ["kmean"])
    S.add("dve", lambda e: e.memset(kmeanb[:], 0.0), [], ["kmeanb"])

    S.barrier()

    def stage_env(ph):
        def sbp(name, shape, dt=F32):
            return ph.enter_context(nc.sbuf_tensor("w_%d_" % len(S.ops["pe"]) + name, list(shape), dt))

        def psp(name, shape, dt=F32):
            return ph.enter_context(nc.psum_tensor("pw_%d_" % len(S.ops["pe"]) + name, list(shape), dt))

        xt = sbp("xt", [128, 4, D])
        xn = sbp("xn", [128, 4, D], BF16)
        junk = sbp("junk", [128, D], BF16)
        ss = sbp("ss", [128, 4]); rstd = sbp("rstd", [128, 4])
        hT = sbp("hT", [128, 8, 512], BF16)
        aT = sbp("aT", [128, NFC, 512], BF16)
        wg = [sbp("wg%d" % i, [128, 8, 256], BF16) for i in range(2)]
        wu = [sbp("wu%d" % i, [128, 8, 256], BF16) for i in range(2)]
        wd = sbp("wd", [128, NFC, D], BF16)
        sg = [sbp("sg%d" % i, [128, 512]) for i in range(2)]
        win = [sbp("win%d" % i, [128, 8, 512], BF16) for i in range(2)]
        zt = [sbp("zt%d" % i, [128, 512]) for i in range(2)]
        zsq = sbp("zsq", [128, 512])
        ssq = sbp("ssq", [128, 8]); rsq = sbp("rsq", [128, 8])
        znb = sbp("znb", [128, 512], BF16)
        zbf = [sbp("zbf%d" % i, [128, 512], BF16) for i in range(2)]
        tst = sbp("tst", [64, 8, 512], BF16)
        gmk = sbp("gmk", [128, 3, 16])
        gv = sbp("gv", [128, 8, 16]); top8 = sbp("top8", [128, 8, 8])
        sel = sbp("sel", [128, 8, 16]); mbb = sbp("mbb", [128, 8, 16], BF16)
        mbT = sbp("mbT", [16, 8, 512], BF16)
        fT = [sbp("fT%d" % i, [128, 512]) for i in range(2)]

        p_t = psp("p_t", [128, 512], BF16)
        p_g = [psp("p_g%d" % i, [128, 512]) for i in range(2)]
        p_u = [psp("p_u%d" % i, [128, 512]) for i in range(2)]
        p_y = p_u
        p_m = psp("p_m", [128, 8, 16])

        p_tk = psp("p_tk", [64, 8, 128], BF16)
        def rmsnorm_T(ns, gain, gname, tag):
            nt = ns * 128
            for s in range(ns):
                S.add("act", lambda e, s=s: e.activation(out=junk[:], in_=xt[:, s, :], func=AF.Square,
                                                         accum_out=ss[:, s:s + 1]),
                      ["xt"], ["junk", "ss"])
            S.add("act", lambda e: e.activation(out=rstd[:, 0:ns], in_=ss[:, 0:ns], func=AF.Sqrt,
                                                scale=1.0 / D, bias=EPS), ["ss"], ["rstd"])
            S.add("dve", lambda e: e.reciprocal(out=rstd[:, 0:ns], in_=rstd[:, 0:ns]), ["rstd"], ["rstd"])
            for s in range(ns):
                S.add("dve", lambda e, s=s: e.scalar_tensor_tensor(out=xn[:, s, :], in0=xt[:, s, :],
                                                                   scalar=rstd[:, s:s + 1], in1=gain[:],
                                                                   op0=ALU.mult, op1=ALU.mult),
                      ["xt", "rstd", gname], [("xn", s)])
            for dc in range(8):
                for s in range(ns):
                    S.add("pe", lambda e, s=s, dc=dc: e.transpose(out=p_t[:, s * 128:(s + 1) * 128],
                                                                  in_=xn[:, s, dc * 128:(dc + 1) * 128],
                                                                  identity=identb[:]),
                          [("xn", s), "identb"], ["p_t"])
                evac(hT[:, dc, 0:nt], p_t[:, 0:nt], ["p_t"], [("hT", dc)])

        def ffn(ns, wgd, wud, wdd, tag):
            nt = ns * 128

            def load_gu(fgp):
                sl = fgp % 2
                S.add("pool", lambda e: e.dma_start(
                    out=wg[sl][:], in_=wgd[:, fgp * 256:(fgp + 1) * 256].rearrange("(c p) n -> p c n", p=128)),
                    [], [("wg", sl)], dma="wg%d" % sl)
                S.add("pool", lambda e: e.dma_start(
                    out=wu[sl][:], in_=wud[:, fgp * 256:(fgp + 1) * 256].rearrange("(c p) n -> p c n", p=128)),
                    [], [("wu", sl)], dma="wu%d" % sl)

            load_gu(0)
            for q4 in range(2):
                f0 = q4 * 11
                S.add("pool", lambda e, f0=f0: e.dma_start(
                    out=wd[:, f0:f0 + 11, :],
                    in_=wdd[f0 * 128:(f0 + 11) * 128, :].rearrange("(c p) n -> p c n", p=128)),
                    [], [("wd", q4)], dma="wd%d" % q4)
            for fgp in range(NFC // 2):
                sl = fgp % 2
                if fgp + 1 < NFC // 2:
                    load_gu(fgp + 1)
                for j in range(2):
                    fc = fgp * 2 + j
                    b = fc % 2
                    for dc in range(8):
                        S.add("pe", lambda e, dc=dc, j=j, sl=sl, b=b: e.matmul(
                            p_g[b][:, 0:nt], lhsT=wg[sl][:, dc, j * 128:(j + 1) * 128], rhs=hT[:, dc, 0:nt],
                            start=(dc == 0), stop=(dc == 7)), [("wg", sl), ("hT", dc)], [("p_g", b)])
                    for dc in range(8):
                        S.add("pe", lambda e, dc=dc, j=j, sl=sl, b=b: e.matmul(
                            p_u[b][:, 0:nt], lhsT=wu[sl][:, dc, j * 128:(j + 1) * 128], rhs=hT[:, dc, 0:nt],
                            start=(dc == 0), stop=(dc == 7)), [("wu", sl), ("hT", dc)], [("p_u", b)])
                    S.add("act", lambda e, b=b: e.activation(out=sg[b][:, 0:nt], in_=p_g[b][:, 0:nt], func=AF.Silu),
                          [("p_g", b)], [("sg", b)])
                    S.add("dve", lambda e, b=b, fc=fc: e.tensor_tensor(out=aT[:, fc, 0:nt], in0=sg[b][:, 0:nt],
                                                                       in1=p_u[b][:, 0:nt], op=ALU.mult),
                          [("sg", b), ("p_u", b)], [("aT", fc)])
            k = 0
            for s in range(ns):
                for hf in range(2):
                    b = k % 2
                    k += 1
                    for fc in range(NFC):
                        S.add("pe", lambda e, fc=fc, s=s, hf=hf, b=b: e.matmul(
                            p_y[b][:], lhsT=aT[:, fc, s * 128:(s + 1) * 128], rhs=wd[:, fc, hf * 512:(hf + 1) * 512],
                            start=(fc == 0), stop=(fc == NFC - 1)), [("aT", fc), ("wd", 0), ("wd", 1)], [("p_u", b)])
                    S.add("dve", lambda e, s=s, hf=hf, b=b: e.scalar_tensor_tensor(
                        out=xt[:, s, hf * 512:(hf + 1) * 512], in0=p_y[b][:], scalar=0.5,
                        in1=xt[:, s, hf * 512:(hf + 1) * 512], op0=ALU.mult, op1=ALU.add),
                        [("p_u", b), "xt"], ["xt"])

        def qknorm(pz, gt, gname, out_f32, zname, oname):
            S.add("act", lambda e: e.activation(out=zsq[:], in_=pz[:], func=AF.Square), [zname], ["zsq"])
            S.add("dve", lambda e: e.tensor_reduce(out=ssq[:], in_=zsq[:].rearrange("p (h d) -> p h d", h=8),
                                                   axis=AX.X, op=ALU.add), ["zsq"], ["ssq"])
            S.add("act", lambda e: e.activation(out=rsq[:], in_=ssq[:], func=AF.Sqrt, scale=1.0 / 64, bias=EPS),
                  ["ssq"], ["rsq"])
            S.add("dve", lambda e: e.reciprocal(out=rsq[:], in_=rsq[:]), ["rsq"], ["rsq"])
            o3 = out_f32.rearrange("p (h d) -> p h d", h=8)
            S.add("dve", lambda e: e.tensor_tensor(out=o3, in0=pz[:].rearrange("p (h d) -> p h d", h=8),
                                                   in1=rsq[:].unsqueeze(2).to_broadcast([128, 8, 64]), op=ALU.mult),
                  [zname, "rsq"], [oname])
            S.add("dve", lambda e: e.tensor_tensor(out=o3, in0=o3, in1=gt[:], op=ALU.mult), [oname, gname], [oname])

        def gating(s, t0q):
            S.add("sp", lambda e: e.dma_start(out=gmk[:], in_=gm_in[t0q + s * 128:t0q + (s + 1) * 128, :, :]),
                  [], ["gmk"], dma="gmk")
            for h in range(8):
                S.add("pe", lambda e, h=h: e.matmul(
                    p_m[:, h, :], lhsT=tst[:, h, s * 128:(s + 1) * 128], rhs=kmeanb[:, h, :],
                    start=True, stop=True), ["tst", "kmeanb"], ["p_m"])
            S.add("dve", lambda e: e.tensor_tensor(
                out=gv[:], in0=p_m[:], in1=gmk[:, 0:1, :].to_broadcast([128, 8, 16]), op=ALU.add),
                ["p_m", "gmk"], ["gv"])
            for h in range(8):
                S.add("dve", lambda e, h=h: e.max(out=top8[:, h, :], in_=gv[:, h, :]), ["gv"], ["top8"])
            S.add("dve", lambda e: e.tensor_tensor(
                out=sel[:], in0=gv[:], in1=top8[:, :, 2:3].to_broadcast([128, 8, 16]), op=ALU.is_ge),
                ["gv", "top8"], ["sel"])
            S.add("dve", lambda e: e.tensor_scalar(out=sel[:], in0=sel[:], scalar1=-1.0, scalar2=-NEGB,
                                                   op0=ALU.add, op1=ALU.mult), ["sel"], ["sel"])
            S.add("dve", lambda e: e.tensor_tensor(
                out=sel[:], in0=sel[:], in1=gmk[:, 1:2, :].to_broadcast([128, 8, 16]), op=ALU.mult),
                ["sel", "gmk"], ["sel"])
            S.add("dve", lambda e: e.tensor_tensor(
                out=mbb[:], in0=sel[:], in1=gmk[:, 2:3, :].to_broadcast([128, 8, 16]), op=ALU.add),
                ["sel", "gmk"], ["mbb"])
            for h in range(8):
                S.add("pe", lambda e, h=h: e.transpose(out=p_tk[0:16, h, :], in_=mbb[:, h, :], identity=identb[:]),
                      ["mbb", "identb"], ["p_tk"])
            evac(mbT[:, :, s * 128:(s + 1) * 128], p_tk[0:16, :, :], ["p_tk"], ["mbT"])

        def stage_a(kind, r0, ns, t0k, t0q):
            nt = ns * 128
            full = kind != "pre"
            dma("sp", xt[:, 0:ns, :], xin[r0:r0 + nt, :].rearrange("(s p) d -> p s d", p=128), [], ["xt"], "xt")
            rmsnorm_T(ns, g1, "g1", "a")
            ffn(ns, w1g, w1u, w1d, "f1")
            if full:
                dma("sp", x1s[t0q:t0q + nt, :].rearrange("(s p) d -> p s d", p=128), xt[:, 0:ns, :],
                    ["xt"], ["x1s"], "x1s")
            rmsnorm_T(ns, gmx, "gmx", "b")
            groups = [1, 0, 2, 3, 4, 5, 6] if full else [1, 2, 4, 5]

            def load_win(gi):
                g = groups[gi]
                sl = gi % 2
                S.add("pool", lambda e: e.dma_start(
                    out=win[sl][:], in_=w_in[:, g * 512:(g + 1) * 512].rearrange("(c p) n -> p c n", p=128)),
                    [], [("win", sl)], dma="win%d" % sl)

            load_win(0)
            for gi, g in enumerate(groups):
                sl = gi % 2
                if gi + 1 < len(groups):
                    load_win(gi + 1)
                if g in (0, 1, 2, 5):
                    for s in range(ns):
                        b = s % 2
                        pz = p_y[b]
                        for dc in range(8):
                            S.add("pe", lambda e, dc=dc, s=s, sl=sl, pz=pz: e.matmul(
                                pz[:], lhsT=hT[:, dc, s * 128:(s + 1) * 128], rhs=win[sl][:, dc, :],
                                start=(dc == 0), stop=(dc == 7)), [("hT", dc), ("win", sl)], [("p_u", b)])
                        if g == 2:
                            evac(zt[b][:], pz[:], [("p_u", b)], [("zt", b)])
                            if kind == "own":
                                dma("sp", v_own[t0q + s * 128:t0q + (s + 1) * 128, :], zt[b][:], [("zt", b)], [], "vo")
                            elif kind == "smp":
                                dma("sp", v_smp[:, :], zt[b][0:NSAMP, :], [("zt", b)], [], "vo")
                            S.add("pool", lambda e, b=b, s=s: e.dma_start(
                                out=v_s[t0k + s * 128:t0k + (s + 1) * 128, :], in_=zt[b][:]),
                                [("zt", b)], ["v_s"], dma="vs")
                        elif g == 5:
                            evac(zbf[b][:], pz[:], [("p_u", b)], [("zbf", b)])
                            dma("sp", ib_s[t0k + s * 128:t0k + (s + 1) * 128, :], zbf[b][:], [("zbf", b)], ["ib_s"], "ibs")
                        else:
                            gt, gname = (gq, "gq") if g == 0 else (gk, "gk")
                            qknorm(pz, gt, gname, zt[b][:], ("p_u", b), ("zt", b))
                            if g == 1:
                                if kind == "own":
                                    dma("sp", k_own[t0q + s * 128:t0q + (s + 1) * 128, :], zt[b][:], [("zt", b)], [], "ko")
                                elif kind == "smp":
                                    dma("sp", k_smp[:, :], zt[b][0:NSAMP, :], [("zt", b)], [], "ko")
                            S.add("act", lambda e, b=b, g=g: e.activation(out=znb[:], in_=zt[b][:], func=AF.Copy,
                                                                          scale=(0.125 if g == 0 else 1.0)),
                                  [("zt", b)], ["znb"])
                            for h in range(8):
                                S.add("pe", lambda e, h=h: e.transpose(
                                    out=p_tk[0:64, h, :], in_=znb[:, h * 64:(h + 1) * 64], identity=identb[:]),
                                    ["znb", "identb"], ["p_tk"])
                            evac(tst[:, :, s * 128:(s + 1) * 128], p_tk[0:64, :, :], ["p_tk"], ["tst"])
                            if g == 0 and kind == "own":
                                gating(s, t0q)
                    if g == 1:
                        dma("sp", kT_s[:, :, t0k:t0k + nt].rearrange("h d t -> d h t"), tst[:, :, 0:nt],
                            ["tst"], ["kT_s"], "kts")
                        if kind != "smp":
                            nb = nt // 256
                            b0 = t0k // 256
                            S.add("dve", lambda e, nb=nb, b0=b0: e.tensor_reduce(
                                out=kmean[:, :, b0:b0 + nb],
                                in_=tst[:, :, 0:nt].rearrange("p h (b t) -> p h b t", t=256),
                                axis=AX.X, op=ALU.add), ["tst"], ["kmean"])
                            S.add("act", lambda e: e.activation(out=kmeanb[:], in_=kmean[:], func=AF.Copy),
                                  ["kmean"], ["kmeanb"])
                    elif g == 0:
                        dma("sp", qT_s[:, 0:64, t0q:t0q + nt].rearrange("h d t -> d h t"), tst[:, :, 0:nt],
                            ["tst"], ["qT_s"], "qts")
                        if kind == "own":
                            dma("sp", qT_s[:, 64:80, t0q:t0q + nt].rearrange("h d t -> d h t"), mbT[:, :, 0:nt],
                                ["mbT"], ["qT_s"], "qts")
                else:
                    for hh in range(4):
                        b = hh % 2
                        pz = p_g[b]
                        for dc in range(8):
                            S.add("pe", lambda e, dc=dc, hh=hh, sl=sl, pz=pz: e.matmul(
                                pz[:, 0:nt], lhsT=win[sl][:, dc, hh * 128:(hh + 1) * 128], rhs=hT[:, dc, 0:nt],
                                start=(dc == 0), stop=(dc == 7)), [("win", sl), ("hT", dc)], [("p_g", b)])
                        if g == 4:
                            S.add("act", lambda e, b=b, pz=pz: e.activation(out=fT[b][:, 0:nt], in_=pz[:, 0:nt],
                                                                            func=AF.Sigmoid), [("p_g", b)], [("fT", b)])
                            S.add("dve", lambda e, b=b, hh=hh: e.tensor_scalar(
                                out=fT[b][:, 0:nt], in0=fT[b][:, 0:nt], scalar1=oml[:, hh:hh + 1],
                                scalar2=lbv[:, hh:hh + 1], op0=ALU.mult, op1=ALU.add),
                                [("fT", b), "oml", "lbv"], [("fT", b)])
                            dma("sp", fg_s[hh, :, t0k:t0k + nt], fT[b][:, 0:nt], [("fT", b)], ["fg_s"], "fgs")
                        else:
                            dst = qb_s if g == 3 else gb_s
                            S.add("act", lambda e, b=b, pz=pz: e.activation(out=fT[b][:, 0:nt], in_=pz[:, 0:nt],
                                                                            func=AF.Silu), [("p_g", b)], [("fT", b)])
                            dma("sp", dst[hh, :, t0q:t0q + nt], fT[b][:, 0:nt], [("fT", b)], ["hg_s"], "fgs")

        return NS(stage_a=stage_a, rmsnorm_T=rmsnorm_T, ffn=ffn, xt=xt, hT=hT, p_y=p_y, p_g=p_g, win=win, zt=zt)

    def hgrn_phase():
        ph = ExitStack()

        def sbp(name, shape, dt=F32):
            return ph.enter_context(nc.sbuf_tensor("h_" + name, list(shape), dt))

        def psp(name, shape, dt=F32):
            return ph.enter_context(nc.psum_tensor("ph_" + name, list(shape), dt))

        hc = sbp("hc", [128, HC_N])
        og = sbp("og", [128, 4])
        onesf = hc[:, HC_ONES:HC_ONES + 128]
        dma("sp", hc[:], hconst_in, [], ["hc"], "hc")
        dma("sp", og[:], ogain_in, [], ["og"], "hog")
        St = [sbp("St%d" % h, [128, 128]) for h in range(4)]
        Sb = [sbp("Sb%d" % h, [128, 128], BF16) for h in range(4)]
        for h in range(4):
            S.add("dve", lambda e, h=h: e.memset(St[h][:], 0.0), [], [("St", h)])
            S.add("dve", lambda e, h=h: e.memset(Sb[h][:], 0.0), [], [("Sb", h)])
        vt = sbp("vt", [128, 4, 512], BF16)
        fgt = [sbp("fgt%d" % h, [128, 512]) for h in range(4)]
        qbt = [sbp("qbt%d" % h, [128, 512]) for h in range(4)]
        gst = [sbp("gst%d" % h, [128, 512]) for h in range(4)]
        lg = [sbp("lg%d" % h, [128, 512]) for h in range(4)]
        kb = [sbp("kb%d" % h, [128, 512]) for h in range(4)]
        cum = [sbp("cum%d" % h, [128, 512]) for h in range(4)]
        tmp = [sbp("tmp%d" % h, [128, 512]) for h in range(4)]
        kh = [sbp("kh%d" % h, [128, 512], BF16) for h in range(4)]
        qt = [sbp("qt%d" % h, [128, 512], BF16) for h in range(4)]
        kt_ = [sbp("kt%d" % h, [128, 512], BF16) for h in range(4)]
        dlast = [sbp("dl%d" % h, [128, 32]) for h in range(4)]
        khT = [sbp("khT%d" % h, [128, 128], BF16) for h in range(4)]
        vm = [sbp("vm%d" % h, [128, 8, 128], BF16) for h in range(4)]
        Usb = [sbp("Usb%d" % h, [128, 8, 128]) for h in range(4)]
        AT = [sbp("AT%d" % h, [128, 128], BF16) for h in range(4)]
        osb = [sbp("osb%d" % h, [128, 128]) for h in range(4)]
        osq = [sbp("osq%d" % h, [128, 128]) for h in range(4)]
        rs = [sbp("rs%d" % h, [128, 128]) for h in range(4)]
        recb = [sbp("recb%d" % h, [128, 512], BF16) for h in range(4)]
        pU = [psp("pU%d" % i, [128, 8, 128]) for i in range(2)]
        pX = [psp("pX%d" % i, [128, 4, 128]) for i in range(2)]
        pT = psp("pT", [128, 4, 128], BF16)

        for st in range(8):
            own = st >= 4
            CH = 16 if own else 128
            nch = 512 // CH
            col0 = st * 512
            q0 = (st - 4) * 512
            rm = hc[:, HC_RM16:HC_RM16 + 512] if own else hc[:, HC_RM128:HC_RM128 + 512]
            dma("sp", vt[:], ib_s[col0:col0 + 512, :].rearrange("(s p) c -> p s c", p=128), ["ib_s"], ["vt"], "hvt")
            for h in range(4):
                dma("sp", fgt[h][:], fg_s[h, :, col0:col0 + 512], ["fg_s"], [("fgt", h)], "hf%d" % h)
                if own:
                    dma("sp", qbt[h][:], qb_s[h, :, q0:q0 + 512], ["hg_s"], [("qbt", h)], "hq%d" % h)
                    dma("sp", gst[h][:], gb_s[h, :, q0:q0 + 512], ["hg_s"], [("gst", h)], "hg%d" % h)
                S.add("act", lambda e, h=h: e.activation(out=lg[h][:], in_=fgt[h][:], func=AF.Ln),
                      [("fgt", h)], [("lg", h)])
                S.add("dve", lambda e, h=h: e.tensor_scalar(out=kb[h][:], in0=fgt[h][:], scalar1=-1.0, scalar2=1.0,
                                                            op0=ALU.mult, op1=ALU.add), [("fgt", h)], [("kb", h)])
                S.add("dve", lambda e, h=h, rm=rm: e.tensor_tensor_scan(out=cum[h][:], data0=rm, data1=lg[h][:],
                                                                        initial=0.0, op0=ALU.mult, op1=ALU.add),
                      [("lg", h), "hc"], [("cum", h)])
                c3 = cum[h][:].rearrange("p (c t) -> p c t", t=CH)
                S.add("dve", lambda e, h=h, c3=c3, nch=nch, CH=CH: e.tensor_tensor(
                    out=tmp[h][:].rearrange("p (c t) -> p c t", t=CH),
                    in0=c3[:, :, CH - 1:CH].to_broadcast([128, nch, CH]), in1=c3, op=ALU.subtract),
                    [("cum", h)], [("tmp", h)])
                S.add("act", lambda e, h=h: e.activation(out=tmp[h][:], in_=tmp[h][:], func=AF.Exp),
                      [("tmp", h)], [("tmp", h)])
                S.add("dve", lambda e, h=h: e.tensor_tensor(out=kh[h][:], in0=kb[h][:], in1=tmp[h][:], op=ALU.mult),
                      [("kb", h), ("tmp", h)], [("kh", h)])
                S.add("act", lambda e, h=h, c3=c3, nch=nch, CH=CH: e.activation(
                    out=dlast[h][:, 0:nch], in_=c3[:, :, CH - 1], func=AF.Exp), [("cum", h)], [("dlast", h)])
                if own:
                    S.add("act", lambda e, h=h: e.activation(out=tmp[h][:], in_=cum[h][:], func=AF.Exp),
                          [("cum", h), ("kh", h)], [("tmp", h)])
                    S.add("dve", lambda e, h=h: e.tensor_tensor(out=qt[h][:], in0=qbt[h][:], in1=tmp[h][:],
                                                                op=ALU.mult), [("qbt", h), ("tmp", h)], [("qt", h)])
                    S.add("act", lambda e, h=h: e.activation(out=tmp[h][:], in_=cum[h][:], func=AF.Exp, scale=-1.0),
                          [("cum", h), ("qt", h)], [("tmp", h)])
                    S.add("dve", lambda e, h=h: e.tensor_tensor(out=kt_[h][:], in0=kb[h][:], in1=tmp[h][:],
                                                                op=ALU.mult), [("kb", h), ("tmp", h)], [("kt", h)])
            for sub in range(4):
                c0 = sub * 128
                nj = 128 // CH
                for h in range(4):
                    pb = h % 2
                    S.add("pe", lambda e, h=h, c0=c0: e.transpose(out=pT[:, h, :], in_=kh[h][:, c0:c0 + 128],
                                                                  identity=identb[:]),
                          [("kh", h), "identb"], [("pT", h)])
                    evac(khT[h][:], pT[:, h, :], [("pT", h)], [("khT", h)])
                    vsl = vt[:, sub, h * 128:(h + 1) * 128]
                    if own:
                        S.add("pool", lambda e, h=h, vsl=vsl: e.tensor_tensor(
                            out=vm[h][:], in0=vsl.unsqueeze(1).to_broadcast([128, 8, 128]),
                            in1=hc[:, HC_CM16:HC_CM16 + 8].unsqueeze(2).to_broadcast([128, 8, 128]), op=ALU.mult),
                            ["vt", "hc"], [("vm", h)])
                        for hf in range(2):
                            S.add("pe", lambda e, h=h, hf=hf, pb=pb: e.matmul(
                                pU[pb][:, hf * 4:(hf + 1) * 4, :], lhsT=khT[h][:], rhs=vm[h][:, hf * 4:(hf + 1) * 4, :],
                                start=True, stop=True), [("khT", h), ("vm", h)], [("pU", pb)])
                        evac(Usb[h][:], pU[pb][:], [("pU", pb)], [("Usb", h)])
                        S.add("pe", lambda e, h=h, c0=c0, pb=pb: e.matmul(
                            pX[pb][:, 0, :], lhsT=kt_[h][:, c0:c0 + 128], rhs=qt[h][:, c0:c0 + 128],
                            start=True, stop=True), [("kt", h), ("qt", h)], [("psc", pb)])
                        S.add("dve", lambda e, h=h, pb=pb: e.tensor_tensor(
                            out=AT[h][:], in0=pX[pb][:, 0, :], in1=hc[:, HC_M16:HC_M16 + 128], op=ALU.mult),
                            [("psc", pb), "hc"], [("AT", h)])
                        S.add("pe", lambda e, h=h, vsl=vsl, pb=pb: e.matmul(
                            pX[pb][:, 1, :], lhsT=vsl, rhs=AT[h][:], start=True, stop=True),
                            ["vt", ("AT", h)], [("po", pb)])
                        evac(osb[h][:], pX[pb][:, 1, :], [("po", pb)], [("osb", h)], eng="act")
                    else:
                        S.add("pe", lambda e, h=h, vsl=vsl, pb=pb: e.matmul(
                            pU[pb][:, 0, :], lhsT=khT[h][:], rhs=vsl, start=True, stop=True),
                            [("khT", h), "vt"], [("pU", pb)])
                        evac(Usb[h][:, 0, :], pU[pb][:, 0, :], [("pU", pb)], [("Usb", h)])
                for hp in range(2):
                    for j in range(nj):
                        for h in (2 * hp, 2 * hp + 1):
                            pb = h % 2
                            ci = sub * nj + j
                            if own:
                                S.add("pe", lambda e, h=h, j=j, c0=c0, pb=pb: e.matmul(
                                    pX[pb][:, 2, j * 16:(j + 1) * 16], lhsT=Sb[h][:],
                                    rhs=qt[h][:, c0 + j * 16:c0 + (j + 1) * 16], start=True, stop=True),
                                    [("Sb", h), ("qt", h)], [("pi", pb)])
                            S.add("dve", lambda e, h=h, j=j, ci=ci: e.scalar_tensor_tensor(
                                out=St[h][:], in0=St[h][:], scalar=dlast[h][:, ci:ci + 1], in1=Usb[h][:, j, :],
                                op0=ALU.mult, op1=ALU.add), [("St", h), ("dlast", h), ("Usb", h)], [("St", h)])
                            S.add("act", lambda e, h=h: e.activation(out=Sb[h][:], in_=St[h][:], func=AF.Copy),
                                  [("St", h)], [("Sb", h)])
                    if own:
                        for h in (2 * hp, 2 * hp + 1):
                            pb = h % 2
                            S.add("dve", lambda e, h=h, pb=pb: e.tensor_tensor(
                                out=osb[h][:], in0=osb[h][:], in1=pX[pb][:, 2, :], op=ALU.add),
                                [("osb", h), ("pi", pb)], [("osb", h)])
                            S.add("act", lambda e, h=h: e.activation(out=osq[h][:], in_=osb[h][:], func=AF.Square),
                                  [("osb", h)], [("osq", h)])
                            S.add("pe", lambda e, h=h, pb=pb: e.matmul(pX[pb][:, 3, :], lhsT=onesf, rhs=osq[h][:],
                                                                       start=True, stop=True),
                                  ["hc", ("osq", h)], [("pss", pb)])
                            S.add("act", lambda e, h=h, pb=pb: e.activation(out=rs[h][:], in_=pX[pb][:, 3, :],
                                                                            func=AF.Sqrt, scale=1.0 / 128, bias=EPS),
                                  [("pss", pb)], [("rs", h)])
                            S.add("dve", lambda e, h=h: e.reciprocal(out=rs[h][:], in_=rs[h][:]),
                                  [("rs", h)], [("rs", h)])
                            S.add("dve", lambda e, h=h: e.tensor_tensor(out=osb[h][:], in0=osb[h][:], in1=rs[h][:],
                                                                        op=ALU.mult), [("osb", h), ("rs", h)],
                                  [("osb", h)])
                            S.add("dve", lambda e, h=h, c0=c0: e.scalar_tensor_tensor(
                                out=recb[h][:, c0:c0 + 128], in0=osb[h][:], scalar=og[:, h:h + 1],
                                in1=gst[h][:, c0:c0 + 128], op0=ALU.mult, op1=ALU.mult),
                                [("osb", h), "og", ("gst", h)], [("recb", h)])
            if own:
                for h in range(4):
                    dma("sp", mixT_s[4 + h, :, q0:q0 + 512], recb[h][:], [("recb", h)], ["mixT_s"], "hrec")
        for h in range(4):
            dma("sp", st_own[h, :, :], St[h][:], [("St", h)], [], "hst")
        S.barrier()
        ph.close()

    def tvec_setup():
        ph = ExitStack()
        relT = ph.enter_context(nc.sbuf_tensor("tv_relT", [33, 8], F32))
        ohs = ph.enter_context(nc.sbuf_tensor("tv_oh", [33, TV_N], F32))
        tvb = ph.enter_context(nc.sbuf_tensor("tv_b", [8, TV_N], BF16))
        ptv = [ph.enter_context(nc.psum_tensor("tv_p%d" % i, [8, 512], F32)) for i in range(2)]
        S.add("dve", lambda e: e.memset(relT[32:33, :], NEGB), [], ["relT32"])
        dma("sp", relT[0:32, :], relb_in, [], ["relT"], "tv")
        dma("sp", ohs[:], oht_in, [], ["ohs"], "tv2")
        for c in range(TV_N // 512):
            b = c % 2
            S.add("pe", lambda e, c=c, b=b: e.matmul(ptv[b][:], lhsT=relT[:, :], rhs=ohs[:, c * 512:(c + 1) * 512],
                                                     start=True, stop=True), ["relT", "relT32", "ohs"], [("ptv", b)])
            evac(tvb[:, c * 512:(c + 1) * 512], ptv[b][:], [("ptv", b)], ["tvb"])
        dma("sp", tvec_s, tvb[:], ["tvb"], ["tvec_s"], "tv3")
        S.barrier()
        ph.close()

    def attn_phase():
        ph = ExitStack()

        def sbp(name, shape, dt=F32):
            return ph.enter_context(nc.sbuf_tensor("t_" + name, list(shape), dt))

        def psp(name, shape, dt=F32):
            return ph.enter_context(nc.psum_tensor("pt_" + name, list(shape), dt))

        KT = [sbp("KT%d" % i, [80, 4096], BF16) for i in range(2)]
        QT = [sbp("QT%d" % i, [80, NOWN], BF16) for i in range(2)]
        Wp = [sbp("Wp%d" % i, [128, 4481], BF16) for i in range(2)]
        Vst = sbp("Vst", [128, 8, 512], BF16)
        Vaug = sbp("Vaug", [128, 32, 8, 65], BF16)
        Jf = sbp("Jf", [128, 128]); Jb = sbp("Jb", [128, 128], BF16)
        onef = sbp("onef", [128, 64])
        pTs = [sbp("pTs%d" % i, [128, 512], BF16) for i in range(3)]
        oacc = sbp("oacc", [65, 512])
        onb = sbp("onb", [64, 512], BF16)
        pl = [psp("pl%d" % i, [128, 512]) for i in range(3)]
        po = [psp("po%d" % i, [128, 512]) for i in range(2)]
        pbc = psp("pbc", [64, 512])

        dma("sp", Jf[:], hconst_in[:, HC_J:HC_J + 128], [], ["Jf"], "tJ")
        S.add("dve", lambda e: e.tensor_copy(out=Jb[:], in_=Jf[:]), ["Jf"], ["Jb"])
        S.add("dve", lambda e: e.memset(onef[:], 1.0), [], ["onef"])
        S.add("pool", lambda e: e.memset(Vaug[:, :, :, 64:65], 1.0), [], ["Vaug1"])
        for r in range(4):
            dma("sp", Vst[:], v_s[r * 1024:(r + 1) * 1024, :].rearrange("(k p) c -> p k c", p=128),
                ["v_s"], ["Vst"], "tV")
            S.add("pool", lambda e, r=r: e.tensor_copy(
                out=Vaug[:, r * 8:(r + 1) * 8, :, 0:64],
                in_=Vst[:].rearrange("p k (h d) -> p k h d", h=8)), ["Vst"], ["Vaug"])

        def load_head(h):
            hb = h % 2
            dma("sp", KT[hb][0:64, :], kT_s[h, :, 0:4096], ["kT_s"], [("KT", hb)], "tK%d" % hb)
            dma("sp", KT[hb][64:80, :], blkoh_in, [], [("KT", hb)], "tK%d" % hb)
            dma("sp", QT[hb][:, :], qT_s[h, :, 0:NOWN], ["qT_s"], [("QT", hb)], "tQ%d" % hb)
            src = bass.AP(tensor=tvec_s.tensor, offset=h * TV_N, ap=[[1, 128], [1, 4481]])
            dma("sp", Wp[hb][:], src, ["tvec_s"], [("Wp", hb)], "tW%d" % hb)

        load_head(0)
        iters = []
        for h in range(8):
            for j in range(4):
                nkt = 20 + 4 * j
                for kt in range(nkt):
                    iters.append((h, j, kt, nkt))

        def front(idx):
            h, j, kt, nkt = iters[idx]
            hb = h % 2
            lb_ = idx % 3
            if j == 0 and kt == 0 and h + 1 < 8:
                load_head(h + 1)
            m0 = (2048 + 512 * j) - 128 * kt + TV_NEG - 127
            S.add("pe", lambda e: e.matmul(
                pl[lb_][:], lhsT=KT[hb][0:80, kt * 128:(kt + 1) * 128], rhs=QT[hb][0:80, j * 512:(j + 1) * 512],
                start=True, stop=False), [("KT", hb), ("QT", hb)], [("pl", lb_)])
            S.add("pe", lambda e: e.matmul(
                pl[lb_][:], lhsT=Jb[:], rhs=Wp[hb][:, m0:m0 + 512], start=False, stop=True),
                ["Jb", ("Wp", hb)], [("pl", lb_)])
            S.add("act", lambda e: e.activation(out=pTs[lb_][:], in_=pl[lb_][:], func=AF.Exp),
                  [("pl", lb_)], [("pTs", lb_)])

        def back(idx):
            h, j, kt, nkt = iters[idx]
            lb_ = idx % 3
            ob = (h * 4 + j) % 2
            S.add("pe", lambda e: e.matmul(
                po[ob][0:65, :], lhsT=Vaug[:, kt, h, :], rhs=pTs[lb_][:],
                start=(kt == 0), stop=(kt == nkt - 1)), ["Vaug", "Vaug1", ("pTs", lb_)], [("po", ob)])
            if kt == nkt - 1:
                evac(oacc[:], po[ob][0:65, :], [("po", ob)], ["oacc"], eng="act")
                S.add("dve", lambda e: e.reciprocal(out=oacc[64:65, :], in_=oacc[64:65, :]), ["oacc"], ["oaccr"])
                S.add("pe", lambda e: e.matmul(pbc[:], lhsT=onef[64:65, :], rhs=oacc[64:65, :], start=True, stop=True),
                      ["onef", "oaccr"], ["pbc"])
                S.add("dve", lambda e: e.tensor_tensor(out=onb[:], in0=oacc[0:64, :], in1=pbc[:], op=ALU.mult),
                      ["oacc", "pbc"], ["onb"])
                dma("sp", mixT_s[h // 2, (h % 2) * 64:(h % 2) * 64 + 64, j * 512:(j + 1) * 512], onb[:],
                    ["onb"], ["mixT_s"], "tO")

        for idx in range(len(iters) + 1):
            if idx < len(iters):
                front(idx)
            if idx >= 1:
                back(idx - 1)
        S.barrier()
        ph.close()

    def stage_b_all(tilesB):
        ph = ExitStack()
        E = stage_env(ph)
        for hf in range(2):
            S.add("pool", lambda e, hf=hf: e.dma_start(
                out=E.win[hf][:], in_=w_out[:, hf * 512:(hf + 1) * 512].rearrange("(c p) n -> p c n", p=128)),
                [], [("win", hf)], dma="win%d" % hf)
        for (kind, ns, t0q) in tilesB:
            nt = ns * 128
            dma("sp", E.xt[:, 0:ns, :], x1s[t0q:t0q + nt, :].rearrange("(s p) d -> p s d", p=128),
                ["x1s"], ["xt"], "xt")
            dma("sp", E.hT[:, :, 0:nt], mixT_s[:, :, t0q:t0q + nt].rearrange("c p t -> p c t"), ["mixT_s"],
                [("hT", c) for c in range(8)], "bm")
            k = 0
            for s in range(ns):
                for hf in range(2):
                    b = k % 2
                    k += 1
                    for c in range(8):
                        S.add("pe", lambda e, c=c, s=s, hf=hf, b=b: e.matmul(
                            E.p_y[b][:], lhsT=E.hT[:, c, s * 128:(s + 1) * 128], rhs=E.win[hf][:, c, :],
                            start=(c == 0), stop=(c == 7)), [("hT", c), ("win", hf)], [("p_u", b)])
                    S.add("dve", lambda e, s=s, hf=hf, b=b: e.tensor_tensor(
                        out=E.xt[:, s, hf * 512:(hf + 1) * 512], in0=E.p_y[b][:],
                        in1=E.xt[:, s, hf * 512:(hf + 1) * 512], op=ALU.add), [("p_u", b), "xt"], ["xt"])
            E.rmsnorm_T(ns, g2, "g2", "c")
            E.ffn(ns, w2g, w2u, w2d, "f2")
            if kind == "own":
                dma("sp", y_own[t0q:t0q + nt, :].rearrange("(s p) d -> p s d", p=128), E.xt[:, 0:ns, :],
                    ["xt"], [], "by")
            else:
                dma("sp", y_smp[:, :], E.xt[0:NSAMP, 0, :], ["xt"], [], "by")
        S.barrier()
        ph.close()

    def hgrn_sample():
        ph = ExitStack()

        def sbp(name, shape, dt=F32):
            return ph.enter_context(nc.sbuf_tensor("hs_" + name, list(shape), dt))

        def psp(name, shape, dt=F32):
            return ph.enter_context(nc.psum_tensor("phs_" + name, list(shape), dt))

        hc = sbp("hc", [128, HC_N]); og = sbp("og", [128, 4])
        onesf = hc[:, HC_ONES:HC_ONES + 128]
        dma("sp", hc[:], hconst_in, [], ["hc"], "hc")
        dma("sp", og[:], ogain_in, [], ["og"], "hog")
        vts = sbp("vts", [16, 512], BF16)
        dma("sp", vts[:], ib_s[4096:4112, :], [], ["vts"], "hvt")
        pA = psp("pA", [128, 4, 128])
        pB = psp("pB", [128, 4, 128])
        pC = psp("pC", [128, 128], BF16)
        for h in range(4):
            fgt = sbp("fgt%d" % h, [128, 16]); qbt = sbp("qbt%d" % h, [128, 16]); gst = sbp("gst%d" % h, [128, 16])
            lg = sbp("lg%d" % h, [128, 16]); kb = sbp("kb%d" % h, [128, 16]); cum = sbp("cum%d" % h, [128, 16])
            tmp = sbp("tmp%d" % h, [128, 16]); kh = sbp("kh%d" % h, [128, 16], BF16)
            qt = sbp("qt%d" % h, [128, 16], BF16); kt_ = sbp("kt%d" % h, [128, 16], BF16)
            dl = sbp("dl%d" % h, [128, 4]); khT = sbp("khT%d" % h, [16, 128], BF16)
            vm = sbp("vm%d" % h, [16, 4, 128], BF16); S0 = sbp("S0%d" % h, [128, 4, 128])
            S0b = sbp("S0b%d" % h, [128, 4, 128], BF16); AT = sbp("AT%d" % h, [16, 16], BF16)
            osb = sbp("osb%d" % h, [128, 16]); osq = sbp("osq%d" % h, [128, 16]); rs = sbp("rs%d" % h, [128, 16])
            recs = sbp("recs%d" % h, [128, 16], BF16)
            K = lambda n, h=h: (n, h)
            dma("sp", fgt[:], fg_s[h, :, 4096:4112], [], [K("fgt")], "hf%d" % h)
            dma("sp", qbt[:], qb_s[h, :, NOWN:NOWN + 16], [], [K("qbt")], "hq%d" % h)
            dma("sp", gst[:], gb_s[h, :, NOWN:NOWN + 16], [], [K("gst")], "hg%d" % h)
            dma("sp", S0[:], st_in[:, h, :, :].rearrange("i k v -> k i v"), [], [K("S0")], "hs%d" % h)
            S.add("act", lambda e, lg=lg, fgt=fgt: e.activation(out=lg[:], in_=fgt[:], func=AF.Ln), [K("fgt")], [K("lg")])
            S.add("dve", lambda e, kb=kb, fgt=fgt: e.tensor_scalar(out=kb[:], in0=fgt[:], scalar1=-1.0, scalar2=1.0,
                                                                  op0=ALU.mult, op1=ALU.add), [K("fgt")], [K("kb")])
            S.add("dve", lambda e, cum=cum, lg=lg: e.tensor_tensor_scan(
                out=cum[:], data0=hc[:, HC_RM4:HC_RM4 + 16], data1=lg[:], initial=0.0, op0=ALU.mult, op1=ALU.add),
                [K("lg"), "hc"], [K("cum")])
            c3 = cum[:].rearrange("p (c t) -> p c t", t=4)
            S.add("dve", lambda e, tmp=tmp, c3=c3: e.tensor_tensor(
                out=tmp[:].rearrange("p (c t) -> p c t", t=4), in0=c3[:, :, 3:4].to_broadcast([128, 4, 4]), in1=c3,
                op=ALU.subtract), [K("cum")], [K("tmp")])
            S.add("act", lambda e, tmp=tmp: e.activation(out=tmp[:], in_=tmp[:], func=AF.Exp), [K("tmp")], [K("tmp")])
            S.add("dve", lambda e, kh=kh, kb=kb, tmp=tmp: e.tensor_tensor(out=kh[:], in0=kb[:], in1=tmp[:], op=ALU.mult),
                  [K("kb"), K("tmp")], [K("kh")])
            S.add("act", lambda e, dl=dl, c3=c3: e.activation(out=dl[:], in_=c3[:, :, 3], func=AF.Exp),
                  [K("cum")], [K("dl")])
            S.add("act", lambda e, tmp=tmp, cum=cum: e.activation(out=tmp[:], in_=cum[:], func=AF.Exp),
                  [K("cum"), K("kh")], [K("tmp")])
            S.add("dve", lambda e, qt=qt, qbt=qbt, tmp=tmp: e.tensor_tensor(out=qt[:], in0=qbt[:], in1=tmp[:], op=ALU.mult),
                  [K("qbt"), K("tmp")], [K("qt")])
            S.add("act", lambda e, tmp=tmp, cum=cum: e.activation(out=tmp[:], in_=cum[:], func=AF.Exp, scale=-1.0),
                  [K("cum"), K("qt")], [K("tmp")])
            S.add("dve", lambda e, kt_=kt_, kb=kb, tmp=tmp: e.tensor_tensor(out=kt_[:], in0=kb[:], in1=tmp[:], op=ALU.mult),
                  [K("kb"), K("tmp")], [K("kt")])
            S.add("pe", lambda e, kh=kh: e.transpose(out=pC[0:16, :], in_=kh[:, :], identity=identb[:]),
                  [K("kh"), "identb"], ["pC"])
            evac(khT[:], pC[0:16, :], ["pC"], [K("khT")])
            vsl = vts[:, h * 128:(h + 1) * 128]
            S.add("dve", lambda e, vm=vm, vsl=vsl: e.tensor_tensor(
                out=vm[:], in0=vsl.unsqueeze(1).to_broadcast([16, 4, 128]),
                in1=hc[0:16, HC_CM4:HC_CM4 + 4].unsqueeze(2).to_broadcast([16, 4, 128]), op=ALU.mult),
                ["vts", "hc"], [K("vm")])
            S.add("pe", lambda e, khT=khT, vm=vm: e.matmul(pA[:], lhsT=khT[:], rhs=vm[:], start=True, stop=True),
                  [K("khT"), K("vm")], ["pA"])
            S.add("act", lambda e, S0b=S0b, S0=S0: e.activation(out=S0b[:], in_=S0[:], func=AF.Copy),
                  [K("S0")], [K("S0b")])
            S.add("pe", lambda e, kt_=kt_, qt=qt: e.matmul(pB[0:16, 0, 0:16], lhsT=kt_[:], rhs=qt[:], start=True, stop=True),
                  [K("kt"), K("qt")], ["pB0"])
            S.add("dve", lambda e, AT=AT: e.tensor_tensor(out=AT[:], in0=pB[0:16, 0, 0:16], in1=hc[0:16, HC_M4:HC_M4 + 16],
                                                          op=ALU.mult), ["pB0", "hc"], [K("AT")])
            S.add("pe", lambda e, vsl=vsl, AT=AT: e.matmul(pB[:, 1, 0:16], lhsT=vsl, rhs=AT[:], start=True, stop=True),
                  ["vts", K("AT")], ["pB1"])
            for i in range(4):
                S.add("pe", lambda e, i=i, S0b=S0b, qt=qt: e.matmul(pB[:, 2, i * 4:(i + 1) * 4], lhsT=S0b[:, i, :],
                                                                    rhs=qt[:, i * 4:(i + 1) * 4], start=True, stop=True),
                      [K("S0b"), K("qt")], ["pB2"])
            S.add("dve", lambda e, S0=S0, dl=dl: e.tensor_tensor(
                out=S0[:], in0=S0[:], in1=dl[:].unsqueeze(2).to_broadcast([128, 4, 128]), op=ALU.mult),
                [K("S0"), K("dl"), K("S0b")], [K("S0")])
            S.add("dve", lambda e, S0=S0: e.tensor_tensor(out=S0[:], in0=S0[:], in1=pA[:], op=ALU.add),
                  [K("S0"), "pA"], [K("S0")])
            dma("sp", st_smp[:, h, :, :].rearrange("i k v -> k i v"), S0[:], [K("S0")], [], "hst")
            evac(osb[:], pB[:, 1, 0:16], ["pB1"], [K("osb")], eng="act")
            S.add("dve", lambda e, osb=osb: e.tensor_tensor(out=osb[:], in0=osb[:], in1=pB[:, 2, 0:16], op=ALU.add),
                  [K("osb"), "pB2"], [K("osb")])
            S.add("act", lambda e, osq=osq, osb=osb: e.activation(out=osq[:], in_=osb[:], func=AF.Square),
                  [K("osb")], [K("osq")])
            S.add("pe", lambda e, osq=osq: e.matmul(pB[:, 3, 0:16], lhsT=onesf, rhs=osq[:], start=True, stop=True),
                  ["hc", K("osq")], ["pB3"])
            S.add("act", lambda e, rs=rs: e.activation(out=rs[:], in_=pB[:, 3, 0:16], func=AF.Sqrt, scale=1.0 / 128,
                                                       bias=EPS), ["pB3"], [K("rs")])
            S.add("dve", lambda e, rs=rs: e.reciprocal(out=rs[:], in_=rs[:]), [K("rs")], [K("rs")])
            S.add("dve", lambda e, osb=osb, rs=rs: e.tensor_tensor(out=osb[:], in0=osb[:], in1=rs[:], op=ALU.mult),
                  [K("osb"), K("rs")], [K("osb")])
            S.add("dve", lambda e, recs=recs, osb=osb, gst=gst, h=h: e.scalar_tensor_tensor(
                out=recs[:], in0=osb[:], scalar=og[:, h:h + 1], in1=gst[:], op0=ALU.mult, op1=ALU.mult),
                [K("osb"), "og", K("gst")], [K("recs")])
            dma("sp", mixT_s[4 + h, :, NOWN:NOWN + 16], recs[:], [K("recs")], ["mixT_s"], "hrec")
        S.barrier()
        ph.close()

    def attn_sample():
        ph = ExitStack()

        def sbp(name, shape, dt=F32):
            return ph.enter_context(nc.sbuf_tensor("as_" + name, list(shape), dt))

        def psp(name, shape, dt=F32):
            return ph.enter_context(nc.psum_tensor("pas_" + name, list(shape), dt))

        KTs = sbp("KTs", [128, 4, 8196], BF16)
        Vs = sbp("Vs", [128, 64, 512], BF16)
        ohr = sbp("ohr", [33, OHR_N], BF16)
        smc = sbp("smc", [128, 513])
        relT = sbp("relT", [33, 8]); relTx = sbp("relTx", [33, 4, 32], BF16)
        ptb = sbp("ptb", [128, 256], I32); idx = sbp("idx", [128, 256], I32)
        kpg = [sbp("kpg%d" % i, [128, 512]) for i in range(3)]
        vpg = [sbp("vpg%d" % i, [128, 512]) for i in range(3)]
        qz = sbp("qz", [128, 4, 32], BF16)
        ks = sbp("ks", [128, 4, 32]); ksb = sbp("ksb", [128, 4, 32], BF16)
        gsb = sbp("gsb", [32, 32]); top8 = sbp("top8", [32, 8]); mb = sbp("mb", [32, 32])
        Pb = [sbp("Pb%d" % i, [32, 512], BF16) for i in range(2)]
        PTs = [sbp("PTs%d" % i, [128, 4, 32], BF16) for i in range(2)]
        den = sbp("den", [32, 34]); dsum = sbp("dsum", [32, 1])
        Vnew = sbp("Vnew", [4, 512], BF16)
        acc = sbp("acc", [32, 512]); o32 = sbp("o32", [32, 64]); oT = sbp("oT", [64, 32], BF16)
        ptr = [psp("ptr%d" % i, [128, 4, 128]) for i in range(2)]
        pls = [psp("pls%d" % i, [32, 512]) for i in range(2)]
        ppt = psp("ppt", [128, 4, 32], BF16)
        pacc = psp("pacc", [32, 512])
        pgt = psp("pgt", [64, 32])

        dma("sp", ohr[:], ohr_in, [], ["ohr"], "sc0")
        dma("sp", smc[:], smc_in, [], ["smc"], "sc2")
        dma("sp", ptb[:], ptab_in.partition_broadcast(128), [], ["ptb"], "sc3")
        dma("sp", relT[0:32, :], relb_in, [], ["relT"], "sc1")
        S.add("dve", lambda e: e.memset(relT[32:33, :], NEGB), [], ["relT32"])
        S.add("dve", lambda e: e.memset(relTx[:], 0.0), [], ["relTx"])
        for t in range(4):
            S.add("dve", lambda e, t=t: e.tensor_copy(
                out=relTx[:, t, :].rearrange("p (h t) -> p h t", t=4)[:, :, t], in_=relT[:, :]),
                ["relT", "relT32", "relTx"], ["relTx"])
        S.add("dve", lambda e: e.tensor_scalar(out=idx[:], in0=ptb[:], scalar1=128.0, scalar2=smc[:, 0:1],
                                               op0=ALU.mult, op1=ALU.add), ["ptb", "smc"], ["idx"])
        S.add("dve", lambda e: e.memset(den[:], 0.0), [], ["den"])
        pc = 0
        for i in range(4):
            for j in range(64):
                sl = pc % 3
                pb = pc % 2
                pc += 1
                col = i * 64 + j
                S.add("pool", lambda e, sl=sl, col=col: e.indirect_dma_start(
                    out=kpg[sl][:, :], out_offset=None, in_=cache_k[:, :],
                    in_offset=bass.IndirectOffsetOnAxis(ap=idx[:, col:col + 1], axis=0)),
                    ["idx"], [("kpg", sl)], dma="sk%d" % sl)
                S.add("pool", lambda e, sl=sl, col=col: e.indirect_dma_start(
                    out=vpg[sl][:, :], out_offset=None, in_=cache_v[:, :],
                    in_offset=bass.IndirectOffsetOnAxis(ap=idx[:, col:col + 1], axis=0)),
                    ["idx"], [("vpg", sl)], dma="sv%d" % sl)
                for c in range(4):
                    S.add("pe", lambda e, sl=sl, c=c, pb=pb: e.transpose(
                        out=ptr[pb][:, c, :], in_=kpg[sl][:, c * 128:(c + 1) * 128], identity=ident[:]),
                        [("kpg", sl), "ident"], [("ptr", pb)])
                evac(KTs[:, :, j * 128:(j + 1) * 128], ptr[pb][:], [("ptr", pb)], ["KTs"])
                S.add("pool", lambda e, sl=sl, j=j: e.tensor_copy(out=Vs[:, j, :], in_=vpg[sl][:]),
                      [("vpg", sl)], ["Vs"])
            dma("sp", KTs[:, :, 8192:8196],
                kT_s.rearrange("(c hl) d t -> (hl d) c t", hl=2)[:, :, 4096 + 4 * i:4096 + 4 * i + 4],
                [], ["KTs"], "sk9")
            dma("sp", Vnew[:], v_s[4096 + 4 * i:4096 + 4 * i + 4, :], [], ["Vnew"], "sv9")
            S.add("dve", lambda e: e.memset(qz[:], 0.0), [], ["qz"])
            for c in range(4):
                for hl in range(2):
                    dma("sp", qz[hl * 64:(hl + 1) * 64, c, c * 8 + hl * 4:c * 8 + hl * 4 + 4],
                        qT_s[2 * c + hl, 0:64, NOWN + 4 * i:NOWN + 4 * i + 4], ["qz"], ["qzd"], "sq")
            S.add("dve", lambda e: e.tensor_reduce(
                out=ks[:], in_=KTs[:, :, 0:8192].rearrange("p c (b t) -> p c b t", t=256), axis=AX.X, op=ALU.add),
                ["KTs"], ["ks"])
            S.add("act", lambda e: e.activation(out=ksb[:], in_=ks[:], func=AF.Copy), ["ks"], ["ksb"])
            for c in range(4):
                S.add("pe", lambda e, c=c: e.matmul(pgt[0:32, :], lhsT=qz[:, c, :], rhs=ksb[:, c, :],
                                                    start=(c == 0), stop=(c == 3)), ["qz", "qzd", "ksb"], ["pgt"])
            evac(gsb[:], pgt[0:32, :], ["pgt"], ["gsb"], eng="act")
            S.add("dve", lambda e: e.max(out=top8[:], in_=gsb[:]), ["gsb"], ["top8"])
            S.add("dve", lambda e: e.tensor_scalar(out=mb[:], in0=gsb[:], scalar1=top8[:, 2:3], scalar2=None,
                                                   op0=ALU.is_ge), ["gsb", "top8"], ["mb"])
            S.add("dve", lambda e: e.tensor_scalar(out=mb[:], in0=mb[:], scalar1=-1.0, scalar2=-NEGB,
                                                   op0=ALU.add, op1=ALU.mult), ["mb"], ["mb"])
            for g in range(17):
                nk = 512 if g < 16 else 4
                b2 = g % 2
                k0 = g * 512
                for c in range(4):
                    S.add("pe", lambda e, c=c, k0=k0, nk=nk, b2=b2: e.matmul(
                        pls[b2][:, 0:nk], lhsT=qz[:, c, :], rhs=KTs[:, c, k0:k0 + nk], start=(c == 0), stop=False),
                        ["qz", "qzd", "KTs"], [("pls", b2)])
                for t in range(4):
                    S.add("pe", lambda e, t=t, k0=k0, nk=nk, b2=b2: e.matmul(
                        pls[b2][:, 0:nk], lhsT=relTx[:, t, :], rhs=ohr[:, k0 + 3 - t:k0 + 3 - t + nk],
                        start=False, stop=(t == 3)), ["relTx", "ohr"], [("pls", b2)])
                if g < 16:
                    for hf in range(2):
                        blk = 2 * g + hf
                        S.add("act", lambda e, hf=hf, blk=blk, b2=b2: e.activation(
                            out=Pb[b2][:, hf * 256:(hf + 1) * 256], in_=pls[b2][:, hf * 256:(hf + 1) * 256],
                            func=AF.Exp, bias=mb[:, blk:blk + 1], accum_out=den[:, blk:blk + 1]),
                            [("pls", b2), "mb"], [("Pb", b2), "den"])
                    for kq in range(4):
                        S.add("pe", lambda e, kq=kq, b2=b2: e.transpose(
                            out=ppt[:, kq, :], in_=Pb[b2][:, kq * 128:(kq + 1) * 128], identity=identb[0:32, 0:32]),
                            [("Pb", b2), "identb"], ["ppt"])
                    evac(PTs[b2][:], ppt[:], ["ppt"], [("PTs", b2)])
                    for kq in range(4):
                        S.add("pe", lambda e, kq=kq, g=g, b2=b2: e.matmul(
                            pacc[:], lhsT=PTs[b2][:, kq, :], rhs=Vs[:, g * 4 + kq, :],
                            start=(g == 0 and kq == 0), stop=False), [("PTs", b2), "Vs"], ["pacc"])
                else:
                    S.add("act", lambda e, b2=b2: e.activation(
                        out=Pb[b2][:, 0:4], in_=pls[b2][:, 0:4], func=AF.Exp, accum_out=den[:, 32:33]),
                        [("pls", b2)], [("Pb", b2), "den"])
                    S.add("pe", lambda e, b2=b2: e.transpose(out=ppt[0:4, 0, :], in_=Pb[b2][:, 0:4],
                                                             identity=identb[0:32, 0:32]),
                          [("Pb", b2), "identb"], ["ppt"])
                    evac(PTs[b2][0:4, 0, :], ppt[0:4, 0, :], ["ppt"], [("PTs", b2)])
                    S.add("pe", lambda e, b2=b2: e.matmul(pacc[:], lhsT=PTs[b2][0:4, 0, :], rhs=Vnew[:, :],
                                                          start=False, stop=True), [("PTs", b2), "Vnew"], ["pacc"])
            evac(acc[:], pacc[:], ["pacc"], ["acc"], eng="act")
            S.add("dve", lambda e: e.tensor_tensor(out=acc[:], in0=acc[:], in1=smc[0:32, 1:513], op=ALU.mult),
                  ["acc", "smc"], ["acc"])
            S.add("dve", lambda e: e.tensor_reduce(out=o32[:], in_=acc[:].rearrange("p (h d) -> p d h", h=8),
                                                   axis=AX.X, op=ALU.add), ["acc"], ["o32"])
            S.add("dve", lambda e: e.tensor_reduce(out=dsum[:], in_=den[:, 0:33], axis=AX.X, op=ALU.add),
                  ["den"], ["dsum"])
            S.add("dve", lambda e: e.reciprocal(out=dsum[:], in_=dsum[:]), ["dsum"], ["dsum"])
            S.add("dve", lambda e: e.tensor_scalar(out=o32[:], in0=o32[:], scalar1=dsum[:, 0:1], scalar2=None,
                                                   op0=ALU.mult), ["o32", "dsum"], ["o32"])
            S.add("pe", lambda e: e.transpose(out=pgt[:, :], in_=o32[:, :], identity=ident[0:32, 0:32]),
                  ["o32", "ident"], ["pgt"])
            evac(oT[:], pgt[:, :], ["pgt"], ["oT"])
            for c in range(4):
                for hl in range(2):
                    dma("sp", mixT_s[c, hl * 64:(hl + 1) * 64, NOWN + 4 * i:NOWN + 4 * i + 4],
                        oT[:, c * 8 + hl * 4:c * 8 + hl * 4 + 4], ["oT"], ["mixT_s"], "so")
        S.barrier()
        ph.close()

    tiles = []
    for i in range(4):
        tiles.append(("pre", i * 512, 4, i * 512, None))
    for i in range(4):
        tiles.append(("own", NPRE + i * 512, 4, NPRE + i * 512, i * 512))
    tiles.append(("smp", R_S0, 1, 4096, NOWN))
    if stop_after == "a1":
        tiles = [tiles[4]]
    ph = ExitStack()
    EA = stage_env(ph)
    for t in tiles:
        EA.stage_a(*t)
    S.barrier()
    ph.close()
    if stop_after not in ("a1", "a"):
        hgrn_phase()
    if stop_after not in ("a1", "a", "h"):
        tvec_setup()
        attn_phase()
    if stop_after not in ("a1", "a", "h", "t", "b_nosmp"):
        hgrn_sample()
        if with_cache:
            attn_sample()
    if stop_after not in ("a1", "a", "h", "t"):
        tb = [("own", 4, i * 512) for i in range(4)]
        if stop_after != "b_nosmp":
            tb.append(("smp", 1, NOWN))
        stage_b_all(tb)

    S.emit(nc, es)
    es.close()
    return nc


_CACHE = {}


def _rel_bucket_np(dist):
    n = np.maximum(dist, 0).astype(np.int64)
    nf = np.maximum(n, 1).astype(np.float32)
    lg = np.log(nf / np.float32(16.0)).astype(np.float32)
    large = 16 + ((lg / np.float32(np.log(256.0))) * np.float32(16.0)).astype(np.float32).astype(np.int32)
    large = np.minimum(large, 31)
    return np.where(n < 16, n, large).astype(np.int64)


def _static_consts():
    hc = np.zeros((128, HC_N), np.float32)
    t = np.arange(512)
    hc[:, HC_RM16:HC_RM16 + 512] = (t % 16 != 0).astype(np.float32)[None, :]
    hc[:, HC_RM128:HC_RM128 + 512] = (t % 128 != 0).astype(np.float32)[None, :]
    si = np.arange(128)[:, None]
    ti = np.arange(128)[None, :]
    hc[:, HC_M16:HC_M16 + 128] = ((si // 16 == ti // 16) & (si <= ti)).astype(np.float32)
    hc[:, HC_CM16:HC_CM16 + 8] = (si // 16 == np.arange(8)[None, :]).astype(np.float32)
    t4 = np.arange(16)
    hc[:, HC_RM4:HC_RM4 + 16] = (t4 % 4 != 0).astype(np.float32)[None, :]
    s4 = np.arange(16)[:, None]
    hc[0:16, HC_M4:HC_M4 + 16] = ((s4 // 4 == t4[None, :] // 4) & (s4 <= t4[None, :])).astype(np.float32)
    hc[0:16, HC_CM4:HC_CM4 + 4] = (s4 // 4 == np.arange(4)[None, :]).astype(np.float32)
    hc[:, HC_ONES:HC_ONES + 128] = 1.0
    hc[:, HC_J:HC_J + 128] = (si == 127 - ti).astype(np.float32)
    blkoh = (np.arange(4096)[None, :] // 256 == np.arange(16)[:, None]).astype(np.float32).astype(ml_dtypes.bfloat16)
    oht = np.zeros((33, TV_N), np.float32)
    i = np.arange(TV_N)
    oht[32, i < TV_NEG] = 1.0
    bk = _rel_bucket_np(i - TV_NEG)
    pos = i >= TV_NEG
    oht[bk[pos], i[pos]] = 1.0
    ohr = np.zeros((33, OHR_N), np.float32)
    m = np.arange(OHR_N)
    dist = 8195 - m
    posd = dist >= 0
    ohr[32, ~posd] = 1.0
    ohr[_rel_bucket_np(dist[posd]), m[posd]] = 1.0
    smc = np.zeros((128, 513), np.float32)
    smc[:, 0] = np.arange(128)
    rows = np.arange(32)[:, None]
    cols = np.arange(512)[None, :]
    smc[0:32, 1:513] = (cols // 64 == rows // 4).astype(np.float32)
    return hc, blkoh, oht, ohr.astype(ml_dtypes.bfloat16), smc


def _consts(half):
    ident = np.eye(128, dtype=np.float32)
    gm = np.zeros((NOWN, 3, 16), np.float32)
    q = np.arange(NOWN)
    ownblk = 8 + q // 256
    for blk in range(16):
        past = blk < ownblk
        if half == 0:
            past = past & (blk >= 8)
        gm[:, 0, blk] = np.where(past, 0.0, -1e30)
        gm[:, 1, blk] = np.where(past, 1.0, 0.0)
        gm[:, 2, blk] = np.where(past | (blk == ownblk), 0.0, NEGB)
    return ident, gm


def _in_maps(inp, with_cache=True):
    maps = []
    if with_cache:
        ck = np.asarray(inp["cache_k"], dtype=np.float32).reshape(2560 * 128, 512)
        cv = np.asarray(inp["cache_v"], dtype=np.float32).reshape(2560 * 128, 512)
    f = lambda a: np.ascontiguousarray(np.asarray(a, dtype=np.float32))
    xp = f(inp["x_prompt"]); xs = f(inp["x_sample"])
    gains = np.stack([f(inp["ffn1_norm"])[0], f(inp["mix_norm"])[0], f(inp["ffn2_norm"])[0]])
    qkg = np.stack([f(inp["q_norm"])[0], f(inp["k_norm"])[0]])
    hc, blkoh, oht, ohr, smc = _static_consts()
    shared = {
        "ffn1_gate": f(inp["ffn1_gate"])[0], "ffn1_up": f(inp["ffn1_up"])[0], "ffn1_down": f(inp["ffn1_down"])[0],
        "ffn2_gate": f(inp["ffn2_gate"])[0], "ffn2_up": f(inp["ffn2_up"])[0], "ffn2_down": f(inp["ffn2_down"])[0],
        "w_in": f(inp["w_in"])[0], "w_out": f(inp["w_out"])[0], "gains": gains, "qkg": qkg,
        "hconst": hc, "blkoh": blkoh, "oht": oht, "relb": f(inp["rel_bias_table"]),
        "ogain": np.ascontiguousarray(f(inp["hgrn_out_norm"])[0].reshape(4, 128).T),
        "lb_logits": np.ascontiguousarray(f(inp["lb_logits"]).reshape(2, 4, 128).transpose(2, 0, 1)),
    }
    for c in range(8):
        b, half = c // 2, c % 2
        xin = np.zeros((NROWS, D), np.float32)
        if half == 1:
            xin[0:NPRE] = xp[b, 0:NPRE]
        xin[NPRE:NPRE + NOWN] = xp[b, half * 2048:(half + 1) * 2048]
        xin[R_S0:R_S0 + NSAMP] = xs[4 * c:4 * c + 4].reshape(NSAMP, D)
        ident, gm = _consts(half)
        m = dict(shared)
        m.update({"xin": xin, "ident": ident, "gmask": gm,
                  "st_in": np.ascontiguousarray(f(inp["state_hgrn"])[0, 4 * c:4 * c + 4])})
        if with_cache:
            m.update({"cache_k": ck, "cache_v": cv, "ohr": ohr, "smc": smc,
                      "ptab": np.ascontiguousarray(np.asarray(inp["page_table"], dtype=np.int32)[4 * c:4 * c + 4].reshape(1, 256))})
        maps.append(m)
    return maps


def kernel(**inp):
    if "nc" not in _CACHE:
        _CACHE["nc"] = build_program()
    nc = _CACHE["nc"]
    maps = _in_maps(inp)
    res = run_bass_kernel_spmd(nc, maps, core_ids=list(range(8)))
    R = res.results
    yp = np.zeros((4, 4096, D), np.float32); kp = np.zeros((1, 4, 4096, 8, 64), np.float32)
    vp = np.zeros((1, 4, 4096, 8, 64), np.float32)
    ys = np.zeros((32, 4, D), np.float32); ks = np.zeros((1, 32, 4, 8, 64), np.float32)
    vs = np.zeros((1, 32, 4, 8, 64), np.float32)
    for c in range(8):
        b, half = c // 2, c % 2
        sl = slice(half * 2048, (half + 1) * 2048)
        yp[b, sl] = R[c]["y_own"]
        kp[0, b, sl] = R[c]["k_own"].reshape(2048, 8, 64)
        vp[0, b, sl] = R[c]["v_own"].reshape(2048, 8, 64)
        ys[4 * c:4 * c + 4] = R[c]["y_smp"].reshape(4, 4, D)
        ks[0, 4 * c:4 * c + 4] = R[c]["k_smp"].reshape(4, 4, 8, 64)
        vs[0, 4 * c:4 * c + 4] = R[c]["v_smp"].reshape(4, 4, 8, 64)
    sp = np.zeros((1, 4, 4, 128, 128), np.float32)
    ssm = np.zeros((1, 32, 4, 128, 128), np.float32)
    for c in range(8):
        if c % 2 == 1:
            sp[0, c // 2] = R[c]["st_own"]
        ssm[0, 4 * c:4 * c + 4] = R[c]["st_smp"]
    return (yp, ys, kp, vp, sp, ks, vs, ssm)
```

```python
import numpy as np
import ml_dtypes
from contextlib import ExitStack
import concourse.bass as bass
import concourse.mybir as mybir
from concourse.bass_utils import run_bass_kernel_spmd

F32 = mybir.dt.float32
BF16 = mybir.dt.bfloat16
I32 = mybir.dt.int32
ALU = mybir.AluOpType
AF = mybir.ActivationFunctionType
AX = mybir.AxisListType

D = 1024
DFF = 2816
NFC = DFF // 128
INW = 3584
NPRE = 2048
NOWN = 2048
NSAMP = 16
R_S0 = NPRE + NOWN
NROWS = R_S0 + 128
EPS = 1e-6
NEGB = -30000.0
ENGS = ["sp", "act", "dve", "pool", "pe"]
HC_RM16 = 0
HC_RM128 = 512
HC_M16 = 1024
HC_CM16 = 1152
HC_RM4 = 1160
HC_M4 = 1176
HC_CM4 = 1192
HC_ONES = 1196
HC_J = 1324
HC_N = 1452
TV_NEG = 512
TV_N = 4608
OHR_N = 8704
NWS = 4


class NS:
    def __init__(self, **kw):
        self.__dict__.update(kw)


class Op:
    __slots__ = ("eng", "fn", "deps", "mark", "dma", "dma_val", "cnt", "idx")


class Sched:
    def __init__(self):
        self.ops = {e: [] for e in ENGS}
        self.lastw = {}
        self.readers = {}
        self.dma_cnt = {}
        self.all_dma = []

    def _dep(self, op, prev, kind):
        if prev is None or prev is op:
            return
        if prev.dma is None and prev.eng == op.eng:
            if op.eng == "pe":
                return
            if kind != "RAW":
                return
        op.deps.add(prev)

    def add(self, eng, fn, reads=(), writes=(), dma=None):
        op = Op()
        op.eng, op.fn, op.deps, op.mark, op.dma = eng, fn, set(), False, dma
        op.dma_val = 0
        op.cnt = 0
        if dma is not None:
            self.dma_cnt[dma] = self.dma_cnt.get(dma, 0) + 16
            op.dma_val = self.dma_cnt[dma]
            self.all_dma.append(op)
        for k in reads:
            self._dep(op, self.lastw.get(k), "RAW")
        for k in writes:
            self._dep(op, self.lastw.get(k), "WAW")
            for r in self.readers.get(k, ()):
                self._dep(op, r, "WAR")
        for k in reads:
            lst = self.readers.setdefault(k, [])
            if op.dma is None:
                lst[:] = [r for r in lst if not (r.dma is None and r.eng == op.eng)]
            lst.append(op)
        for k in writes:
            self.lastw[k] = op
            self.readers[k] = []
        op.idx = len(self.ops[eng])
        self.ops[eng].append(op)
        return op

    def barrier(self):
        lasts = []
        for e in ENGS:
            if self.ops[e]:
                lasts.append(self.ops[e][-1])
        dmas = list(self.all_dma)
        for e in ENGS:
            op = self.add(e, None)
            for l in lasts:
                if l.dma is None and l.eng != e:
                    op.deps.add(l)
            seen = {}
            for d_ in dmas:
                seen[d_.dma] = d_
            for d_ in seen.values():
                op.deps.add(d_)
        self.lastw.clear()
        self.readers.clear()

    def emit(self, nc, es):
        for e in ENGS:
            for op in self.ops[e]:
                for d_ in op.deps:
                    if d_.dma is None:
                        d_.mark = True
        for e in ENGS:
            c = 0
            for op in self.ops[e]:
                if op.dma is None:
                    if op.mark and op.fn is not None:
                        c += 1
                    op.cnt = c
        esem = {e: es.enter_context(nc.semaphore("e_" + e)) for e in ENGS}
        dsem = {k: es.enter_context(nc.semaphore("d_" + k)) for k in self.dma_cnt}
        block = es.enter_context(nc.Block())
        sched = self

        def run(e, eng):
            waited = {}
            for op in sched.ops[e]:
                for d_ in op.deps:
                    if d_.dma is not None:
                        key, val = ("d", d_.dma), d_.dma_val
                        sem = dsem[d_.dma]
                    else:
                        key, val = ("e", d_.eng), d_.cnt
                        sem = esem[d_.eng]
                    if val <= 0 or waited.get(key, 0) >= val:
                        continue
                    waited[key] = val
                    eng.wait_ge(sem, val)
                if op.fn is None:
                    continue
                ins = op.fn(eng)
                if op.dma is not None:
                    ins.then_inc(dsem[op.dma], 16)
                elif op.mark:
                    ins.then_inc(esem[e], 1)
            if e == "sp":
                for k, v in sched.dma_cnt.items():
                    eng.wait_ge(dsem[k], v)

        @block.sync
        def _(eng):
            run("sp", eng)

        @block.scalar
        def _(eng):
            run("act", eng)

        @block.vector
        def _(eng):
            run("dve", eng)

        @block.gpsimd
        def _(eng):
            run("pool", eng)

        @block.tensor
        def _(eng):
            run("pe", eng)


def build_program(stop_after=None, dbg=False, with_cache=True):
    nc = bass.Bass("TRN2", target_bir_lowering=False)
    es = ExitStack()
    S = Sched()

    def din(name, shape, dt=F32):
        return nc.dram_tensor(name, list(shape), dt, kind="ExternalInput").ap()

    def dout(name, shape, dt=F32):
        return nc.dram_tensor(name, list(shape), dt, kind="ExternalOutput").ap()

    def dscr(name, shape, dt=F32):
        if dbg:
            return nc.dram_tensor(name, list(shape), dt, kind="ExternalOutput").ap()
        return nc.dram_tensor(name, list(shape), dt).ap()

    def sb(name, shape, dt=F32):
        return es.enter_context(nc.sbuf_tensor("s_" + name, list(shape), dt))

    def ps(name, shape, dt=F32):
        return es.enter_context(nc.psum_tensor(name, list(shape), dt))

    xin = din("xin", [NROWS, D])
    w1g = din("ffn1_gate", [D, DFF]); w1u = din("ffn1_up", [D, DFF]); w1d = din("ffn1_down", [DFF, D])
    w2g = din("ffn2_gate", [D, DFF]); w2u = din("ffn2_up", [D, DFF]); w2d = din("ffn2_down", [DFF, D])
    w_in = din("w_in", [D, INW]); w_out = din("w_out", [D, D])
    gains = din("gains", [3, D])
    qkg = din("qkg", [2, 64])
    lbl = din("lb_logits", [128, 2, 4])
    ident_in = din("ident", [128, 128])
    gm_in = din("gmask", [NOWN, 3, 16])

    hconst_in = din("hconst", [128, HC_N])
    ogain_in = din("ogain", [128, 4])
    blkoh_in = din("blkoh", [16, 4096], BF16)
    oht_in = din("oht", [33, TV_N])
    relb_in = din("relb", [32, 8])
    st_own = dout("st_own", [4, 128, 128])
    mixT_s = dscr("mixT_s", [8, 128, NOWN + 128], BF16)
    tvec_s = dscr("tvec_s", [8, TV_N], BF16)

    st_in = din("st_in", [4, 4, 128, 128])
    st_smp = dout("st_smp", [4, 4, 128, 128])
    if with_cache:
        cache_k = din("cache_k", [2560 * 128, 512])
        cache_v = din("cache_v", [2560 * 128, 512])
        ptab_in = din("ptab", [1, 256], I32)
        ohr_in = din("ohr", [33, OHR_N], BF16)
        smc_in = din("smc", [128, 513])

    y_own = dout("y_own", [NOWN, D])
    y_smp = dout("y_smp", [NSAMP, D])
    k_own = dout("k_own", [NOWN, 512]); v_own = dout("v_own", [NOWN, 512])
    k_smp = dout("k_smp", [NSAMP, 512]); v_smp = dout("v_smp", [NSAMP, 512])

    x1s = dscr("x1s", [NOWN + 128, D])
    kT_s = dscr("kT_s", [8, 64, 4096 + 128], BF16)
    qT_s = dscr("qT_s", [8, 80, NOWN + 128], BF16)
    v_s = dscr("v_s", [4096 + 128, 512], BF16)
    fg_s = dscr("fg_s", [4, 128, 4096 + 128])
    qb_s = dscr("qb_s", [4, 128, NOWN + 128])
    gb_s = dscr("gb_s", [4, 128, NOWN + 128])
    ib_s = dscr("ib_s", [4096 + 128, 512], BF16)

    ident = sb("ident", [128, 128])
    identb = sb("identb", [128, 128], BF16)
    g1 = sb("g1", [128, D]); gmx = sb("gmx", [128, D]); g2 = sb("g2", [128, D])
    gq = sb("gq", [128, 8, 64]); gk = sb("gk", [128, 8, 64])
    lbt = sb("lbt", [128, 2, 4])
    lbv = sb("lbv", [128, 4]); oml = sb("oml", [128, 4])
    kmean = sb("kmean", [64, 8, 16]); kmeanb = sb("kmeanb", [64, 8, 16], BF16)

    rr = {"ev": 0}

    def evac(out, in_, reads, writes, eng=None):
        if eng is None:
            eng = "act" if rr["ev"] % 2 == 0 else "dve"
            rr["ev"] += 1
        if eng == "act":
            S.add("act", lambda e: e.activation(out=out, in_=in_, func=AF.Copy), reads, writes)
        else:
            S.add(eng, lambda e: e.tensor_copy(out=out, in_=in_), reads, writes)

    def dma(q, out, in_, reads, writes, key):
        S.add(q, lambda e: e.dma_start(out=out, in_=in_), reads, writes, dma=key)

    dma("sp", ident[:], ident_in, [], ["ident"], "c0")
    S.add("dve", lambda e: e.tensor_copy(out=identb[:], in_=ident[:]), ["ident"], ["identb"])
    for i, (t, nm) in enumerate([(g1, "g1"), (gmx, "gmx"), (g2, "g2")]):
        dma("sp", t[:], gains[i:i + 1, :].partition_broadcast(128), [], [nm], "c0")
    for i, (t, nm) in enumerate([(gq, "gq"), (gk, "gk")]):
        for h in range(8):
            dma("sp", t[:, h, :], qkg[i:i + 1, :].partition_broadcast(128), [], [nm], "c0")
    dma("sp", lbt[:], lbl, [], ["lbt"], "c0")
    S.add("dve", lambda e: e.tensor_tensor(out=lbv[:], in0=lbt[:, 0, :], in1=lbt[:, 1, :], op=ALU.subtract),
          ["lbt"], ["lbv"])
    S.add("act", lambda e: e.activation(out=lbv[:], in_=lbv[:], func=AF.Sigmoid), ["lbv"], ["lbv"])
    S.add("dve", lambda e: e.tensor_scalar(out=oml[:], in0=lbv[:], scalar1=-1.0, scalar2=1.0,
                                           op0=ALU.mult, op1=ALU.add), ["lbv"], ["oml"])
    S.add("dve", lambda e: e.memset(kmean[:], 0.0), [], ["kmean"])
    S.add("dve", lambda e: e.memset(kmeanb[:], 0.0), [], ["kmeanb"])

    S.barrier()

    def stage_env(ph):
        def sbp(name, shape, dt=F32):
            return ph.enter_context(nc.sbuf_tensor("w_%d_" % len(S.ops["pe"]) + name, list(shape), dt))

        def psp(name, shape, dt=F32):
            return ph.enter_context(nc.psum_tensor("pw_%d_" % len(S.ops["pe"]) + name, list(shape), dt))

        xt = sbp("xt", [128, 4, D])
        xn = sbp("xn", [128, 4, D], BF16)
        junk = sbp("junk", [128, D], BF16)
        ss = sbp("ss", [128, 4]); rstd = sbp("rstd", [128, 4])
        hT = sbp("hT", [128, 8, 512], BF16)
        aT = sbp("aT", [128, NFC, 512], BF16)
        wg = [sbp("wg%d" % i, [128, 8, 256], BF16) for i in range(NWS)]
        wu = [sbp("wu%d" % i, [128, 8, 256], BF16) for i in range(NWS)]
        wd = sbp("wd", [128, NFC, D], BF16)
        sg = [sbp("sg%d" % i, [128, 512]) for i in range(2)]
        win = [sbp("win%d" % i, [128, 8, 512], BF16) for i in range(2)]
        zt = [sbp("zt%d" % i, [128, 512]) for i in range(2)]
        zsq = sbp("zsq", [128, 512])
        ssq = sbp("ssq", [128, 8]); rsq = sbp("rsq", [128, 8])
        znb = sbp("znb", [128, 512], BF16)
        zbf = [sbp("zbf%d" % i, [128, 512], BF16) for i in range(2)]
        tst = sbp("tst", [64, 8, 512], BF16)
        gmk = sbp("gmk", [128, 3, 16])
        gv = sbp("gv", [128, 8, 16]); top8 = sbp("top8", [128, 8, 8])
        sel = sbp("sel", [128, 8, 16]); mbb = sbp("mbb", [128, 8, 16], BF16)
        mbT = sbp("mbT", [16, 8, 512], BF16)
        fT = [sbp("fT%d" % i, [128, 512]) for i in range(2)]

        p_t = psp("p_t", [128, 512], BF16)
        p_g = [psp("p_g%d" % i, [128, 512]) for i in range(2)]
        p_u = [psp("p_u%d" % i, [128, 512]) for i in range(2)]
        p_y = p_u
        p_m = psp("p_m", [128, 8, 16])

        p_tk = psp("p_tk", [64, 8, 128], BF16)
        def rmsnorm_T(ns, gain, gname, tag):
            nt = ns * 128
            for s in range(ns):
                S.add("act", lambda e, s=s: e.activation(out=junk[:], in_=xt[:, s, :], func=AF.Square,
                                                         accum_out=ss[:, s:s + 1]),
                      ["xt"], ["junk", "ss"])
            S.add("act", lambda e: e.activation(out=rstd[:, 0:ns], in_=ss[:, 0:ns], func=AF.Sqrt,
                                                scale=1.0 / D, bias=EPS), ["ss"], ["rstd"])
            S.add("dve", lambda e: e.reciprocal(out=rstd[:, 0:ns], in_=rstd[:, 0:ns]), ["rstd"], ["rstd"])
            for s in range(ns):
                S.add("dve", lambda e, s=s: e.scalar_tensor_tensor(out=xn[:, s, :], in0=xt[:, s, :],
                                                                   scalar=rstd[:, s:s + 1], in1=gain[:],
                                                                   op0=ALU.mult, op1=ALU.mult),
                      ["xt", "rstd", gname], [("xn", s)])
            for dc in range(8):
                for s in range(ns):
                    S.add("pe", lambda e, s=s, dc=dc: e.transpose(out=p_t[:, s * 128:(s + 1) * 128],
                                                                  in_=xn[:, s, dc * 128:(dc + 1) * 128],
                                                                  identity=identb[:]),
                          [("xn", s), "identb"], ["p_t"])
                evac(hT[:, dc, 0:nt], p_t[:, 0:nt], ["p_t"], [("hT", dc)])

        pre = {"gu": None, "wd": None}

        def load_gu_w(wgd, wud, fgp):
            sl = fgp % NWS
            S.add("pool", lambda e: e.dma_start(
                out=wg[sl][:], in_=wgd[:, fgp * 256:(fgp + 1) * 256].rearrange("(c p) n -> p c n", p=128)),
                [], [("wg", sl)], dma="wg%d" % sl)
            S.add("pool", lambda e: e.dma_start(
                out=wu[sl][:], in_=wud[:, fgp * 256:(fgp + 1) * 256].rearrange("(c p) n -> p c n", p=128)),
                [], [("wu", sl)], dma="wu%d" % sl)

        def load_wd_w(wdd):
            for q4 in range(2):
                f0 = q4 * 11
                S.add("pool", lambda e, f0=f0: e.dma_start(
                    out=wd[:, f0:f0 + 11, :],
                    in_=wdd[f0 * 128:(f0 + 11) * 128, :].rearrange("(c p) n -> p c n", p=128)),
                    [], [("wd", q4)], dma="wd%d" % q4)

        def ffn(ns, wgd, wud, wdd, tag, nxt=None, mid_hook=None):
            nt = ns * 128

            def load_gu(fgp):
                load_gu_w(wgd, wud, fgp)

            if pre["gu"] is not wgd:
                for f_ in range(NWS - 1):
                    load_gu(f_)
            pre["gu"] = None
            if pre["wd"] is not wdd:
                load_wd_w(wdd)
            pre["wd"] = None
            for fgp in range(NFC // 2):
                sl = fgp % NWS
                if fgp + NWS - 1 < NFC // 2:
                    load_gu(fgp + NWS - 1)
                for j in range(2):
                    fc = fgp * 2 + j
                    b = fc % 2
                    for dc in range(8):
                        S.add("pe", lambda e, dc=dc, j=j, sl=sl, b=b: e.matmul(
                            p_g[b][:, 0:nt], lhsT=wg[sl][:, dc, j * 128:(j + 1) * 128], rhs=hT[:, dc, 0:nt],
                            start=(dc == 0), stop=(dc == 7)), [("wg", sl), ("hT", dc)], [("p_g", b)])
                    for dc in range(8):
                        S.add("pe", lambda e, dc=dc, j=j, sl=sl, b=b: e.matmul(
                            p_u[b][:, 0:nt], lhsT=wu[sl][:, dc, j * 128:(j + 1) * 128], rhs=hT[:, dc, 0:nt],
                            start=(dc == 0), stop=(dc == 7)), [("wu", sl), ("hT", dc)], [("p_u", b)])
                    S.add("act", lambda e, b=b: e.activation(out=sg[b][:, 0:nt], in_=p_g[b][:, 0:nt], func=AF.Silu),
                          [("p_g", b)], [("sg", b)])
                    S.add("dve", lambda e, b=b, fc=fc: e.tensor_tensor(out=aT[:, fc, 0:nt], in0=sg[b][:, 0:nt],
                                                                       in1=p_u[b][:, 0:nt], op=ALU.mult),
                          [("sg", b), ("p_u", b)], [("aT", fc)])
            if mid_hook is not None:
                mid_hook()
            if nxt is not None:
                for f_ in range(NWS - 1):
                    load_gu_w(nxt[0], nxt[1], f_)
                pre["gu"] = nxt[0]
            k = 0
            for s in range(ns):
                for hf in range(2):
                    b = k % 2
                    k += 1
                    for fc in range(NFC):
                        S.add("pe", lambda e, fc=fc, s=s, hf=hf, b=b: e.matmul(
                            p_y[b][:], lhsT=aT[:, fc, s * 128:(s + 1) * 128], rhs=wd[:, fc, hf * 512:(hf + 1) * 512],
                            start=(fc == 0), stop=(fc == NFC - 1)), [("aT", fc), ("wd", 0), ("wd", 1)], [("p_u", b)])
                    S.add("dve", lambda e, s=s, hf=hf, b=b: e.scalar_tensor_tensor(
                        out=xt[:, s, hf * 512:(hf + 1) * 512], in0=p_y[b][:], scalar=0.5,
                        in1=xt[:, s, hf * 512:(hf + 1) * 512], op0=ALU.mult, op1=ALU.add),
                        [("p_u", b), "xt"], ["xt"])

        def prefetch_wd(wdd):
            load_wd_w(wdd)
            pre["wd"] = wdd

        def qknorm(pz, gt, gname, out_f32, zname, oname):
            S.add("act", lambda e: e.activation(out=zsq[:], in_=pz[:], func=AF.Square), [zname], ["zsq"])
            S.add("dve", lambda e: e.tensor_reduce(out=ssq[:], in_=zsq[:].rearrange("p (h d) -> p h d", h=8),
                                                   axis=AX.X, op=ALU.add), ["zsq"], ["ssq"])
            S.add("act", lambda e: e.activation(out=rsq[:], in_=ssq[:], func=AF.Sqrt, scale=1.0 / 64, bias=EPS),
                  ["ssq"], ["rsq"])
            S.add("dve", lambda e: e.reciprocal(out=rsq[:], in_=rsq[:]), ["rsq"], ["rsq"])
            o3 = out_f32.rearrange("p (h d) -> p h d", h=8)
            S.add("dve", lambda e: e.tensor_tensor(out=o3, in0=pz[:].rearrange("p (h d) -> p h d", h=8),
                                                   in1=rsq[:].unsqueeze(2).to_broadcast([128, 8, 64]), op=ALU.mult),
                  [zname, "rsq"], [oname])
            S.add("dve", lambda e: e.tensor_tensor(out=o3, in0=o3, in1=gt[:], op=ALU.mult), [oname, gname], [oname])

        def gating(s, t0q):
            S.add("sp", lambda e: e.dma_start(out=gmk[:], in_=gm_in[t0q + s * 128:t0q + (s + 1) * 128, :, :]),
                  [], ["gmk"], dma="gmk")
            for h in range(8):
                S.add("pe", lambda e, h=h: e.matmul(
                    p_m[:, h, :], lhsT=tst[:, h, s * 128:(s + 1) * 128], rhs=kmeanb[:, h, :],
                    start=True, stop=True), ["tst", "kmeanb"], ["p_m"])
            S.add("dve", lambda e: e.tensor_tensor(
                out=gv[:], in0=p_m[:], in1=gmk[:, 0:1, :].to_broadcast([128, 8, 16]), op=ALU.add),
                ["p_m", "gmk"], ["gv"])
            for h in range(8):
                S.add("dve", lambda e, h=h: e.max(out=top8[:, h, :], in_=gv[:, h, :]), ["gv"], ["top8"])
            S.add("dve", lambda e: e.tensor_tensor(
                out=sel[:], in0=gv[:], in1=top8[:, :, 2:3].to_broadcast([128, 8, 16]), op=ALU.is_ge),
                ["gv", "top8"], ["sel"])
            S.add("dve", lambda e: e.tensor_scalar(out=sel[:], in0=sel[:], scalar1=-1.0, scalar2=-NEGB,
                                                   op0=ALU.add, op1=ALU.mult), ["sel"], ["sel"])
            S.add("dve", lambda e: e.tensor_tensor(
                out=sel[:], in0=sel[:], in1=gmk[:, 1:2, :].to_broadcast([128, 8, 16]), op=ALU.mult),
                ["sel", "gmk"], ["sel"])
            S.add("dve", lambda e: e.tensor_tensor(
                out=mbb[:], in0=sel[:], in1=gmk[:, 2:3, :].to_broadcast([128, 8, 16]), op=ALU.add),
                ["sel", "gmk"], ["mbb"])
            for h in range(8):
                S.add("pe", lambda e, h=h: e.transpose(out=p_tk[0:16, h, :], in_=mbb[:, h, :], identity=identb[:]),
                      ["mbb", "identb"], ["p_tk"])
            evac(mbT[:, :, s * 128:(s + 1) * 128], p_tk[0:16, :, :], ["p_tk"], ["mbT"])

        def stage_a(kind, r0, ns, t0k, t0q, last=False):
            nt = ns * 128
            full = kind != "pre"
            dma("sp", xt[:, 0:ns, :], xin[r0:r0 + nt, :].rearrange("(s p) d -> p s d", p=128), [], ["xt"], "xt")
            groups = [1, 0, 2, 3, 4, 5, 6] if full else [1, 2, 4, 5]

            def load_win(gi):
                g = groups[gi]
                sl = gi % 2
                S.add("pool", lambda e: e.dma_start(
                    out=win[sl][:], in_=w_in[:, g * 512:(g + 1) * 512].rearrange("(c p) n -> p c n", p=128)),
                    [], [("win", sl)], dma="win%d" % sl)

            rmsnorm_T(ns, g1, "g1", "a")
            ffn(ns, w1g, w1u, w1d, "f1", nxt=None if last else (w1g, w1u, w1d),
                mid_hook=lambda: (load_win(0), load_win(1)))
            if full:
                dma("sp", x1s[t0q:t0q + nt, :].rearrange("(s p) d -> p s d", p=128), xt[:, 0:ns, :],
                    ["xt"], ["x1s"], "x1s")
            rmsnorm_T(ns, gmx, "gmx", "b")
            for gi, g in enumerate(groups):
                sl = gi % 2
                if g in (0, 1, 2, 5):
                    for s in range(ns):
                        b = s % 2
                        pz = p_y[b]
                        for dc in range(8):
                            S.add("pe", lambda e, dc=dc, s=s, sl=sl, pz=pz: e.matmul(
                                pz[:], lhsT=hT[:, dc, s * 128:(s + 1) * 128], rhs=win[sl][:, dc, :],
                                start=(dc == 0), stop=(dc == 7)), [("hT", dc), ("win", sl)], [("p_u", b)])
                        if g == 2:
                            evac(zt[b][:], pz[:], [("p_u", b)], [("zt", b)])
                            if kind == "own":
                                dma("sp", v_own[t0q + s * 128:t0q + (s + 1) * 128, :], zt[b][:], [("zt", b)], [], "vo")
                            elif kind == "smp":
                                dma("sp", v_smp[:, :], zt[b][0:NSAMP, :], [("zt", b)], [], "vo")
                            S.add("pool", lambda e, b=b, s=s: e.dma_start(
                                out=v_s[t0k + s * 128:t0k + (s + 1) * 128, :], in_=zt[b][:]),
                                [("zt", b)], ["v_s"], dma="vs")
                        elif g == 5:
                            evac(zbf[b][:], pz[:], [("p_u", b)], [("zbf", b)])
                            dma("sp", ib_s[t0k + s * 128:t0k + (s + 1) * 128, :], zbf[b][:], [("zbf", b)], ["ib_s"], "ibs")
                        else:
                            gt, gname = (gq, "gq") if g == 0 else (gk, "gk")
                            qknorm(pz, gt, gname, zt[b][:], ("p_u", b), ("zt", b))
                            if g == 1:
                                if kind == "own":
                                    dma("sp", k_own[t0q + s * 128:t0q + (s + 1) * 128, :], zt[b][:], [("zt", b)], [], "ko")
                                elif kind == "smp":
                                    dma("sp", k_smp[:, :], zt[b][0:NSAMP, :], [("zt", b)], [], "ko")
                            S.add("act", lambda e, b=b, g=g: e.activation(out=znb[:], in_=zt[b][:], func=AF.Copy,
                                                                          scale=(0.125 if g == 0 else 1.0)),
                                  [("zt", b)], ["znb"])
                            for h in range(8):
                                S.add("pe", lambda e, h=h: e.transpose(
                                    out=p_tk[0:64, h, :], in_=znb[:, h * 64:(h + 1) * 64], identity=identb[:]),
                                    ["znb", "identb"], ["p_tk"])
                            evac(tst[:, :, s * 128:(s + 1) * 128], p_tk[0:64, :, :], ["p_tk"], ["tst"])
                            if g == 0 and kind == "own":
                                gating(s, t0q)
                    if g == 1:
                        dma("sp", kT_s[:, :, t0k:t0k + nt].rearrange("h d t -> d h t"), tst[:, :, 0:nt],
                            ["tst"], ["kT_s"], "kts")
                        if kind != "smp":
                            nb = nt // 256
                            b0 = t0k // 256
                            S.add("dve", lambda e, nb=nb, b0=b0: e.tensor_reduce(
                                out=kmean[:, :, b0:b0 + nb],
                                in_=tst[:, :, 0:nt].rearrange("p h (b t) -> p h b t", t=256),
                                axis=AX.X, op=ALU.add), ["tst"], ["kmean"])
                            S.add("act", lambda e: e.activation(out=kmeanb[:], in_=kmean[:], func=AF.Copy),
                                  ["kmean"], ["kmeanb"])
                    elif g == 0:
                        dma("sp", qT_s[:, 0:64, t0q:t0q + nt].rearrange("h d t -> d h t"), tst[:, :, 0:nt],
                            ["tst"], ["qT_s"], "qts")
                        if kind == "own":
                            dma("sp", qT_s[:, 64:80, t0q:t0q + nt].rearrange("h d t -> d h t"), mbT[:, :, 0:nt],
                                ["mbT"], ["qT_s"], "qts")
                else:
                    for hh in range(4):
                        b = hh % 2
                        pz = p_g[b]
                        for dc in range(8):
                            S.add("pe", lambda e, dc=dc, hh=hh, sl=sl, pz=pz: e.matmul(
                                pz[:, 0:nt], lhsT=win[sl][:, dc, hh * 128:(hh + 1) * 128], rhs=hT[:, dc, 0:nt],
                                start=(dc == 0), stop=(dc == 7)), [("win", sl), ("hT", dc)], [("p_g", b)])
                        if g == 4:
                            S.add("act", lambda e, b=b, pz=pz: e.activation(out=fT[b][:, 0:nt], in_=pz[:, 0:nt],
                                                                            func=AF.Sigmoid), [("p_g", b)], [("fT", b)])
                            S.add("dve", lambda e, b=b, hh=hh: e.tensor_scalar(
                                out=fT[b][:, 0:nt], in0=fT[b][:, 0:nt], scalar1=oml[:, hh:hh + 1],
                                scalar2=lbv[:, hh:hh + 1], op0=ALU.mult, op1=ALU.add),
                                [("fT", b), "oml", "lbv"], [("fT", b)])
                            dma("sp", fg_s[hh, :, t0k:t0k + nt], fT[b][:, 0:nt], [("fT", b)], ["fg_s"], "fgs")
                        else:
                            dst = qb_s if g == 3 else gb_s
                            S.add("act", lambda e, b=b, pz=pz: e.activation(out=fT[b][:, 0:nt], in_=pz[:, 0:nt],
                                                                            func=AF.Silu), [("p_g", b)], [("fT", b)])
                            dma("sp", dst[hh, :, t0q:t0q + nt], fT[b][:, 0:nt], [("fT", b)], ["hg_s"], "fgs")
                if gi + 2 < len(groups):
                    load_win(gi + 2)
            if not last:
                prefetch_wd(w1d)

        return NS(stage_a=stage_a, rmsnorm_T=rmsnorm_T, ffn=ffn, xt=xt, hT=hT, p_y=p_y, p_g=p_g, win=win, zt=zt,
                  prefetch_wd=prefetch_wd)

    def hgrn_phase():
        ph = ExitStack()

        def sbp(name, shape, dt=F32):
            return ph.enter_context(nc.sbuf_tensor("h_" + name, list(shape), dt))

        def psp(name, shape, dt=F32):
            return ph.enter_context(nc.psum_tensor("ph_" + name, list(shape), dt))

        hc = sbp("hc", [128, HC_N])
        og = sbp("og", [128, 4])
        onesf = hc[:, HC_ONES:HC_ONES + 128]
        dma("sp", hc[:], hconst_in, [], ["hc"], "hc")
        dma("sp", og[:], ogain_in, [], ["og"], "hog")
        St = [sbp("St%d" % h, [128, 128]) for h in range(4)]
        Sb = [sbp("Sb%d" % h, [128, 128], BF16) for h in range(4)]
        for h in range(4):
            S.add("dve", lambda e, h=h: e.memset(St[h][:], 0.0), [], [("St", h)])
            S.add("dve", lambda e, h=h: e.memset(Sb[h][:], 0.0), [], [("Sb", h)])
        vt = sbp("vt", [128, 4, 512], BF16)
        fgt = [sbp("fgt%d" % h, [128, 512]) for h in range(4)]
        qbt = [sbp("qbt%d" % h, [128, 512]) for h in range(4)]
        gst = [sbp("gst%d" % h, [128, 512]) for h in range(4)]
        lg = [sbp("lg%d" % h, [128, 512]) for h in range(4)]
        kb = [sbp("kb%d" % h, [128, 512]) for h in range(4)]
        cum = [sbp("cum%d" % h, [128, 512]) for h in range(4)]
        tmp = [sbp("tmp%d" % h, [128, 512]) for h in range(4)]
        kh = [sbp("kh%d" % h, [128, 512], BF16) for h in range(4)]
        qt = [sbp("qt%d" % h, [128, 512], BF16) for h in range(4)]
        kt_ = [sbp("kt%d" % h, [128, 512], BF16) for h in range(4)]
        dlast = [sbp("dl%d" % h, [128, 32]) for h in range(4)]
        khT = [sbp("khT%d" % h, [128, 128], BF16) for h in range(4)]
        vm = [sbp("vm%d" % h, [128, 8, 128], BF16) for h in range(4)]
        Usb = [sbp("Usb%d" % h, [128, 8, 128]) for h in range(4)]
        AT = [sbp("AT%d" % h, [128, 128], BF16) for h in range(4)]
        osb = [sbp("osb%d" % h, [128, 128]) for h in range(4)]
        osq = [sbp("osq%d" % h, [128, 128]) for h in range(4)]
        rs = [sbp("rs%d" % h, [128, 128]) for h in range(4)]
        recb = [sbp("recb%d" % h, [128, 512], BF16) for h in range(4)]
        pU = [psp("pU%d" % i, [128, 8, 128]) for i in range(2)]
        pX = [psp("pX%d" % i, [128, 4, 128]) for i in range(2)]
        pT = psp("pT", [128, 4, 128], BF16)

        for st in range(8):
            own = st >= 4
            CH = 16 if own else 128
            nch = 512 // CH
            col0 = st * 512
            q0 = (st - 4) * 512
            rm = hc[:, HC_RM16:HC_RM16 + 512] if own else hc[:, HC_RM128:HC_RM128 + 512]
            dma("sp", vt[:], ib_s[col0:col0 + 512, :].rearrange("(s p) c -> p s c", p=128), ["ib_s"], ["vt"], "hvt")
            for h in range(4):
                dma("sp", fgt[h][:], fg_s[h, :, col0:col0 + 512], ["fg_s"], [("fgt", h)], "hf%d" % h)
                if own:
                    dma("sp", qbt[h][:], qb_s[h, :, q0:q0 + 512], ["hg_s"], [("qbt", h)], "hq%d" % h)
                    dma("sp", gst[h][:], gb_s[h, :, q0:q0 + 512], ["hg_s"], [("gst", h)], "hg%d" % h)
                S.add("act", lambda e, h=h: e.activation(out=lg[h][:], in_=fgt[h][:], func=AF.Ln),
                      [("fgt", h)], [("lg", h)])
                S.add("dve", lambda e, h=h: e.tensor_scalar(out=kb[h][:], in0=fgt[h][:], scalar1=-1.0, scalar2=1.0,
                                                            op0=ALU.mult, op1=ALU.add), [("fgt", h)], [("kb", h)])
                S.add("dve", lambda e, h=h, rm=rm: e.tensor_tensor_scan(out=cum[h][:], data0=rm, data1=lg[h][:],
                                                                        initial=0.0, op0=ALU.mult, op1=ALU.add),
                      [("lg", h), "hc"], [("cum", h)])
                c3 = cum[h][:].rearrange("p (c t) -> p c t", t=CH)
                S.add("dve", lambda e, h=h, c3=c3, nch=nch, CH=CH: e.tensor_tensor(
                    out=tmp[h][:].rearrange("p (c t) -> p c t", t=CH),
                    in0=c3[:, :, CH - 1:CH].to_broadcast([128, nch, CH]), in1=c3, op=ALU.subtract),
                    [("cum", h)], [("tmp", h)])
                S.add("act", lambda e, h=h: e.activation(out=tmp[h][:], in_=tmp[h][:], func=AF.Exp),
                      [("tmp", h)], [("tmp", h)])
                S.add("dve", lambda e, h=h: e.tensor_tensor(out=kh[h][:], in0=kb[h][:], in1=tmp[h][:], op=ALU.mult),
                      [("kb", h), ("tmp", h)], [("kh", h)])
                S.add("act", lambda e, h=h, c3=c3, nch=nch, CH=CH: e.activation(
                    out=dlast[h][:, 0:nch], in_=c3[:, :, CH - 1], func=AF.Exp), [("cum", h)], [("dlast", h)])
                if own:
                    S.add("act", lambda e, h=h: e.activation(out=tmp[h][:], in_=cum[h][:], func=AF.Exp),
                          [("cum", h), ("kh", h)], [("tmp", h)])
                    S.add("dve", lambda e, h=h: e.tensor_tensor(out=qt[h][:], in0=qbt[h][:], in1=tmp[h][:],
                                                                op=ALU.mult), [("qbt", h), ("tmp", h)], [("qt", h)])
                    S.add("act", lambda e, h=h: e.activation(out=tmp[h][:], in_=cum[h][:], func=AF.Exp, scale=-1.0),
                          [("cum", h), ("qt", h)], [("tmp", h)])
                    S.add("dve", lambda e, h=h: e.tensor_tensor(out=kt_[h][:], in0=kb[h][:], in1=tmp[h][:],
                                                                op=ALU.mult), [("kb", h), ("tmp", h)], [("kt", h)])
            for sub in range(4):
                c0 = sub * 128
                nj = 128 // CH
                for h in range(4):
                    pb = h % 2
                    S.add("pe", lambda e, h=h, c0=c0: e.transpose(out=pT[:, h, :], in_=kh[h][:, c0:c0 + 128],
                                                                  identity=identb[:]),
                          [("kh", h), "identb"], [("pT", h)])
                    evac(khT[h][:], pT[:, h, :], [("pT", h)], [("khT", h)])
                    vsl = vt[:, sub, h * 128:(h + 1) * 128]
                    if own:
                        S.add("pool", lambda e, h=h, vsl=vsl: e.tensor_tensor(
                            out=vm[h][:], in0=vsl.unsqueeze(1).to_broadcast([128, 8, 128]),
                            in1=hc[:, HC_CM16:HC_CM16 + 8].unsqueeze(2).to_broadcast([128, 8, 128]), op=ALU.mult),
                            ["vt", "hc"], [("vm", h)])
                        for hf in range(2):
                            S.add("pe", lambda e, h=h, hf=hf, pb=pb: e.matmul(
                                pU[pb][:, hf * 4:(hf + 1) * 4, :], lhsT=khT[h][:], rhs=vm[h][:, hf * 4:(hf + 1) * 4, :],
                                start=True, stop=True), [("khT", h), ("vm", h)], [("pU", pb)])
                        evac(Usb[h][:], pU[pb][:], [("pU", pb)], [("Usb", h)])
                        S.add("pe", lambda e, h=h, c0=c0, pb=pb: e.matmul(
                            pX[pb][:, 0, :], lhsT=kt_[h][:, c0:c0 + 128], rhs=qt[h][:, c0:c0 + 128],
                            start=True, stop=True), [("kt", h), ("qt", h)], [("psc", pb)])
                        S.add("dve", lambda e, h=h, pb=pb: e.tensor_tensor(
                            out=AT[h][:], in0=pX[pb][:, 0, :], in1=hc[:, HC_M16:HC_M16 + 128], op=ALU.mult),
                            [("psc", pb), "hc"], [("AT", h)])
                        S.add("pe", lambda e, h=h, vsl=vsl, pb=pb: e.matmul(
                            pX[pb][:, 1, :], lhsT=vsl, rhs=AT[h][:], start=True, stop=True),
                            ["vt", ("AT", h)], [("po", pb)])
                        evac(osb[h][:], pX[pb][:, 1, :], [("po", pb)], [("osb", h)], eng="act")
                    else:
                        S.add("pe", lambda e, h=h, vsl=vsl, pb=pb: e.matmul(
                            pU[pb][:, 0, :], lhsT=khT[h][:], rhs=vsl, start=True, stop=True),
                            [("khT", h), "vt"], [("pU", pb)])
                        evac(Usb[h][:, 0, :], pU[pb][:, 0, :], [("pU", pb)], [("Usb", h)])
                for hp in range(2):
                    for j in range(nj):
                        for h in (2 * hp, 2 * hp + 1):
                            pb = h % 2
                            ci = sub * nj + j
                            if own:
                                S.add("pe", lambda e, h=h, j=j, c0=c0, pb=pb: e.matmul(
                                    pX[pb][:, 2, j * 16:(j + 1) * 16], lhsT=Sb[h][:],
                                    rhs=qt[h][:, c0 + j * 16:c0 + (j + 1) * 16], start=True, stop=True),
                                    [("Sb", h), ("qt", h)], [("pi", pb)])
                            S.add("dve", lambda e, h=h, j=j, ci=ci: e.scalar_tensor_tensor(
                                out=St[h][:], in0=St[h][:], scalar=dlast[h][:, ci:ci + 1], in1=Usb[h][:, j, :],
                                op0=ALU.mult, op1=ALU.add), [("St", h), ("dlast", h), ("Usb", h)], [("St", h)])
                            S.add("act", lambda e, h=h: e.activation(out=Sb[h][:], in_=St[h][:], func=AF.Copy),
                                  [("St", h)], [("Sb", h)])
                    if own:
                        for h in (2 * hp, 2 * hp + 1):
                            pb = h % 2
                            S.add("dve", lambda e, h=h, pb=pb: e.tensor_tensor(
                                out=osb[h][:], in0=osb[h][:], in1=pX[pb][:, 2, :], op=ALU.add),
                                [("osb", h), ("pi", pb)], [("osb", h)])
                            S.add("act", lambda e, h=h: e.activation(out=osq[h][:], in_=osb[h][:], func=AF.Square),
                                  [("osb", h)], [("osq", h)])
                            S.add("pe", lambda e, h=h, pb=pb: e.matmul(pX[pb][:, 3, :], lhsT=onesf, rhs=osq[h][:],
                                                                       start=True, stop=True),
                                  ["hc", ("osq", h)], [("pss", pb)])
                            S.add("act", lambda e, h=h, pb=pb: e.activation(out=rs[h][:], in_=pX[pb][:, 3, :],
                                                                            func=AF.Sqrt, scale=1.0 / 128, bias=EPS),
                                  [("pss", pb)], [("rs", h)])
                            S.add("dve", lambda e, h=h: e.reciprocal(out=rs[h][:], in_=rs[h][:]),
                                  [("rs", h)], [("rs", h)])
                            S.add("dve", lambda e, h=h: e.tensor_tensor(out=osb[h][:], in0=osb[h][:], in1=rs[h][:],
                                                                        op=ALU.mult), [("osb", h), ("rs", h)],
                                  [("osb", h)])
                            S.add("dve", lambda e, h=h, c0=c0: e.scalar_tensor_tensor(
                                out=recb[h][:, c0:c0 + 128], in0=osb[h][:], scalar=og[:, h:h + 1],
                                in1=gst[h][:, c0:c0 + 128], op0=ALU.mult, op1=ALU.mult),
                                [("osb", h), "og", ("gst", h)], [("recb", h)])
            if own:
                for h in range(4):
                    dma("sp", mixT_s[4 + h, :, q0:q0 + 512], recb[h][:], [("recb", h)], ["mixT_s"], "hrec")
        for h in range(4):
            dma("sp", st_own[h, :, :], St[h][:], [("St", h)], [], "hst")
        S.barrier()
        ph.close()

    def tvec_setup():
        ph = ExitStack()
        relT = ph.enter_context(nc.sbuf_tensor("tv_relT", [33, 8], F32))
        ohs = ph.enter_context(nc.sbuf_tensor("tv_oh", [33, TV_N], F32))
        tvb = ph.enter_context(nc.sbuf_tensor("tv_b", [8, TV_N], BF16))
        ptv = [ph.enter_context(nc.psum_tensor("tv_p%d" % i, [8, 512], F32)) for i in range(2)]
        S.add("dve", lambda e: e.memset(relT[32:33, :], NEGB), [], ["relT32"])
        dma("sp", relT[0:32, :], relb_in, [], ["relT"], "tv")
        dma("sp", ohs[:], oht_in, [], ["ohs"], "tv2")
        for c in range(TV_N // 512):
            b = c % 2
            S.add("pe", lambda e, c=c, b=b: e.matmul(ptv[b][:], lhsT=relT[:, :], rhs=ohs[:, c * 512:(c + 1) * 512],
                                                     start=True, stop=True), ["relT", "relT32", "ohs"], [("ptv", b)])
            evac(tvb[:, c * 512:(c + 1) * 512], ptv[b][:], [("ptv", b)], ["tvb"])
        dma("sp", tvec_s, tvb[:], ["tvb"], ["tvec_s"], "tv3")
        S.barrier()
        ph.close()

    def attn_phase():
        ph = ExitStack()

        def sbp(name, shape, dt=F32):
            return ph.enter_context(nc.sbuf_tensor("t_" + name, list(shape), dt))

        def psp(name, shape, dt=F32):
            return ph.enter_context(nc.psum_tensor("pt_" + name, list(shape), dt))

        KT = [sbp("KT%d" % i, [80, 4096], BF16) for i in range(2)]
        QT = [sbp("QT%d" % i, [80, NOWN], BF16) for i in range(2)]
        Wp = [sbp("Wp%d" % i, [128, 4481], BF16) for i in range(2)]
        Vst = sbp("Vst", [128, 8, 512], BF16)
        Vaug = sbp("Vaug", [128, 32, 8, 65], BF16)
        Jf = sbp("Jf", [128, 128]); Jb = sbp("Jb", [128, 128], BF16)
        onef = sbp("onef", [128, 64])
        pTs = [sbp("pTs%d" % i, [128, 512], BF16) for i in range(3)]
        oacc = sbp("oacc", [65, 512])
        onb = sbp("onb", [64, 512], BF16)
        pl = [psp("pl%d" % i, [128, 512]) for i in range(3)]
        po = [psp("po%d" % i, [128, 512]) for i in range(2)]
        pbc = psp("pbc", [64, 512])

        dma("sp", Jf[:], hconst_in[:, HC_J:HC_J + 128], [], ["Jf"], "tJ")
        S.add("dve", lambda e: e.tensor_copy(out=Jb[:], in_=Jf[:]), ["Jf"], ["Jb"])
        S.add("dve", lambda e: e.memset(onef[:], 1.0), [], ["onef"])
        S.add("pool", lambda e: e.memset(Vaug[:, :, :, 64:65], 1.0), [], ["Vaug1"])
        for r in range(4):
            dma("sp", Vst[:], v_s[r * 1024:(r + 1) * 1024, :].rearrange("(k p) c -> p k c", p=128),
                ["v_s"], ["Vst"], "tV")
            S.add("pool", lambda e, r=r: e.tensor_copy(
                out=Vaug[:, r * 8:(r + 1) * 8, :, 0:64],
                in_=Vst[:].rearrange("p k (h d) -> p k h d", h=8)), ["Vst"], ["Vaug"])

        def load_head(h):
            hb = h % 2
            dma("sp", KT[hb][0:64, :], kT_s[h, :, 0:4096], ["kT_s"], [("KT", hb)], "tK%d" % hb)
            dma("sp", KT[hb][64:80, :], blkoh_in, [], [("KT", hb)], "tK%d" % hb)
            dma("sp", QT[hb][:, :], qT_s[h, :, 0:NOWN], ["qT_s"], [("QT", hb)], "tQ%d" % hb)
            src = bass.AP(tensor=tvec_s.tensor, offset=h * TV_N, ap=[[1, 128], [1, 4481]])
            dma("sp", Wp[hb][:], src, ["tvec_s"], [("Wp", hb)], "tW%d" % hb)

        load_head(0)
        iters = []
        for h in range(8):
            for j in range(4):
                nkt = 20 + 4 * j
                for kt in range(nkt):
                    iters.append((h, j, kt, nkt))

        def front(idx):
            h, j, kt, nkt = iters[idx]
            hb = h % 2
            lb_ = idx % 3
            if j == 0 and kt == 0 and h + 1 < 8:
                load_head(h + 1)
            m0 = (2048 + 512 * j) - 128 * kt + TV_NEG - 127
            S.add("pe", lambda e: e.matmul(
                pl[lb_][:], lhsT=KT[hb][0:80, kt * 128:(kt + 1) * 128], rhs=QT[hb][0:80, j * 512:(j + 1) * 512],
                start=True, stop=False), [("KT", hb), ("QT", hb)], [("pl", lb_)])
            S.add("pe", lambda e: e.matmul(
                pl[lb_][:], lhsT=Jb[:], rhs=Wp[hb][:, m0:m0 + 512], start=False, stop=True),
                ["Jb", ("Wp", hb)], [("pl", lb_)])
            S.add("act", lambda e: e.activation(out=pTs[lb_][:], in_=pl[lb_][:], func=AF.Exp),
                  [("pl", lb_)], [("pTs", lb_)])

        def back(idx):
            h, j, kt, nkt = iters[idx]
            lb_ = idx % 3
            ob = (h * 4 + j) % 2
            S.add("pe", lambda e: e.matmul(
                po[ob][0:65, :], lhsT=Vaug[:, kt, h, :], rhs=pTs[lb_][:],
                start=(kt == 0), stop=(kt == nkt - 1)), ["Vaug", "Vaug1", ("pTs", lb_)], [("po", ob)])
            if kt == nkt - 1:
                evac(oacc[:], po[ob][0:65, :], [("po", ob)], ["oacc"], eng="act")
                S.add("dve", lambda e: e.reciprocal(out=oacc[64:65, :], in_=oacc[64:65, :]), ["oacc"], ["oaccr"])
                S.add("pe", lambda e: e.matmul(pbc[:], lhsT=onef[64:65, :], rhs=oacc[64:65, :], start=True, stop=True),
                      ["onef", "oaccr"], ["pbc"])
                S.add("dve", lambda e: e.tensor_tensor(out=onb[:], in0=oacc[0:64, :], in1=pbc[:], op=ALU.mult),
                      ["oacc", "pbc"], ["onb"])
                dma("sp", mixT_s[h // 2, (h % 2) * 64:(h % 2) * 64 + 64, j * 512:(j + 1) * 512], onb[:],
                    ["onb"], ["mixT_s"], "tO")

        for idx in range(len(iters) + 1):
            if idx < len(iters):
                front(idx)
            if idx >= 1:
                back(idx - 1)
        S.barrier()
        ph.close()

    def stage_b_all(tilesB):
        ph = ExitStack()
        E = stage_env(ph)
        for hf in range(2):
            S.add("pool", lambda e, hf=hf: e.dma_start(
                out=E.win[hf][:], in_=w_out[:, hf * 512:(hf + 1) * 512].rearrange("(c p) n -> p c n", p=128)),
                [], [("win", hf)], dma="win%d" % hf)
        for bi, (kind, ns, t0q) in enumerate(tilesB):
            nt = ns * 128
            dma("sp", E.xt[:, 0:ns, :], x1s[t0q:t0q + nt, :].rearrange("(s p) d -> p s d", p=128),
                ["x1s"], ["xt"], "xt")
            dma("sp", E.hT[:, :, 0:nt], mixT_s[:, :, t0q:t0q + nt].rearrange("c p t -> p c t"), ["mixT_s"],
                [("hT", c) for c in range(8)], "bm")
            k = 0
            for s in range(ns):
                for hf in range(2):
                    b = k % 2
                    k += 1
                    for c in range(8):
                        S.add("pe", lambda e, c=c, s=s, hf=hf, b=b: e.matmul(
                            E.p_y[b][:], lhsT=E.hT[:, c, s * 128:(s + 1) * 128], rhs=E.win[hf][:, c, :],
                            start=(c == 0), stop=(c == 7)), [("hT", c), ("win", hf)], [("p_u", b)])
                    S.add("dve", lambda e, s=s, hf=hf, b=b: e.tensor_tensor(
                        out=E.xt[:, s, hf * 512:(hf + 1) * 512], in0=E.p_y[b][:],
                        in1=E.xt[:, s, hf * 512:(hf + 1) * 512], op=ALU.add), [("p_u", b), "xt"], ["xt"])
            E.rmsnorm_T(ns, g2, "g2", "c")
            E.ffn(ns, w2g, w2u, w2d, "f2", nxt=None if bi == len(tilesB) - 1 else (w2g, w2u, w2d))
            if bi < len(tilesB) - 1:
                E.prefetch_wd(w2d)
            if kind == "own":
                dma("sp", y_own[t0q:t0q + nt, :].rearrange("(s p) d -> p s d", p=128), E.xt[:, 0:ns, :],
                    ["xt"], [], "by")
            else:
                dma("sp", y_smp[:, :], E.xt[0:NSAMP, 0, :], ["xt"], [], "by")
        S.barrier()
        ph.close()

    def hgrn_sample():
        ph = ExitStack()

        def sbp(name, shape, dt=F32):
            return ph.enter_context(nc.sbuf_tensor("hs_" + name, list(shape), dt))

        def psp(name, shape, dt=F32):
            return ph.enter_context(nc.psum_tensor("phs_" + name, list(shape), dt))

        hc = sbp("hc", [128, HC_N]); og = sbp("og", [128, 4])
        onesf = hc[:, HC_ONES:HC_ONES + 128]
        dma("sp", hc[:], hconst_in, [], ["hc"], "hc")
        dma("sp", og[:], ogain_in, [], ["og"], "hog")
        vts = sbp("vts", [16, 512], BF16)
        dma("sp", vts[:], ib_s[4096:4112, :], [], ["vts"], "hvt")
        pA = psp("pA", [128, 4, 128])
        pB = psp("pB", [128, 4, 128])
        pC = psp("pC", [128, 128], BF16)
        for h in range(4):
            fgt = sbp("fgt%d" % h, [128, 16]); qbt = sbp("qbt%d" % h, [128, 16]); gst = sbp("gst%d" % h, [128, 16])
            lg = sbp("lg%d" % h, [128, 16]); kb = sbp("kb%d" % h, [128, 16]); cum = sbp("cum%d" % h, [128, 16])
            tmp = sbp("tmp%d" % h, [128, 16]); kh = sbp("kh%d" % h, [128, 16], BF16)
            qt = sbp("qt%d" % h, [128, 16], BF16); kt_ = sbp("kt%d" % h, [128, 16], BF16)
            dl = sbp("dl%d" % h, [128, 4]); khT = sbp("khT%d" % h, [16, 128], BF16)
            vm = sbp("vm%d" % h, [16, 4, 128], BF16); S0 = sbp("S0%d" % h, [128, 4, 128])
            S0b = sbp("S0b%d" % h, [128, 4, 128], BF16); AT = sbp("AT%d" % h, [16, 16], BF16)
            osb = sbp("osb%d" % h, [128, 16]); osq = sbp("osq%d" % h, [128, 16]); rs = sbp("rs%d" % h, [128, 16])
            recs = sbp("recs%d" % h, [128, 16], BF16)
            K = lambda n, h=h: (n, h)
            dma("sp", fgt[:], fg_s[h, :, 4096:4112], [], [K("fgt")], "hf%d" % h)
            dma("sp", qbt[:], qb_s[h, :, NOWN:NOWN + 16], [], [K("qbt")], "hq%d" % h)
            dma("sp", gst[:], gb_s[h, :, NOWN:NOWN + 16], [], [K("gst")], "hg%d" % h)
            dma("sp", S0[:], st_in[:, h, :, :].rearrange("i k v -> k i v"), [], [K("S0")], "hs%d" % h)
            S.add("act", lambda e, lg=lg, fgt=fgt: e.activation(out=lg[:], in_=fgt[:], func=AF.Ln), [K("fgt")], [K("lg")])
            S.add("dve", lambda e, kb=kb, fgt=fgt: e.tensor_scalar(out=kb[:], in0=fgt[:], scalar1=-1.0, scalar2=1.0,
                                                                  op0=ALU.mult, op1=ALU.add), [K("fgt")], [K("kb")])
            S.add("dve", lambda e, cum=cum, lg=lg: e.tensor_tensor_scan(
                out=cum[:], data0=hc[:, HC_RM4:HC_RM4 + 16], data1=lg[:], initial=0.0, op0=ALU.mult, op1=ALU.add),
                [K("lg"), "hc"], [K("cum")])
            c3 = cum[:].rearrange("p (c t) -> p c t", t=4)
            S.add("dve", lambda e, tmp=tmp, c3=c3: e.tensor_tensor(
                out=tmp[:].rearrange("p (c t) -> p c t", t=4), in0=c3[:, :, 3:4].to_broadcast([128, 4, 4]), in1=c3,
                op=ALU.subtract), [K("cum")], [K("tmp")])
            S.add("act", lambda e, tmp=tmp: e.activation(out=tmp[:], in_=tmp[:], func=AF.Exp), [K("tmp")], [K("tmp")])
            S.add("dve", lambda e, kh=kh, kb=kb, tmp=tmp: e.tensor_tensor(out=kh[:], in0=kb[:], in1=tmp[:], op=ALU.mult),
                  [K("kb"), K("tmp")], [K("kh")])
            S.add("act", lambda e, dl=dl, c3=c3: e.activation(out=dl[:], in_=c3[:, :, 3], func=AF.Exp),
                  [K("cum")], [K("dl")])
            S.add("act", lambda e, tmp=tmp, cum=cum: e.activation(out=tmp[:], in_=cum[:], func=AF.Exp),
                  [K("cum"), K("kh")], [K("tmp")])
            S.add("dve", lambda e, qt=qt, qbt=qbt, tmp=tmp: e.tensor_tensor(out=qt[:], in0=qbt[:], in1=tmp[:], op=ALU.mult),
                  [K("qbt"), K("tmp")], [K("qt")])
            S.add("act", lambda e, tmp=tmp, cum=cum: e.activation(out=tmp[:], in_=cum[:], func=AF.Exp, scale=-1.0),
                  [K("cum"), K("qt")], [K("tmp")])
            S.add("dve", lambda e, kt_=kt_, kb=kb, tmp=tmp: e.tensor_tensor(out=kt_[:], in0=kb[:], in1=tmp[:], op=ALU.mult),
                  [K("kb"), K("tmp")], [K("kt")])
            S.add("pe", lambda e, kh=kh: e.transpose(out=pC[0:16, :], in_=kh[:, :], identity=identb[:]),
                  [K("kh"), "identb"], ["pC"])
            evac(khT[:], pC[0:16, :], ["pC"], [K("khT")])
            vsl = vts[:, h * 128:(h + 1) * 128]
            S.add("dve", lambda e, vm=vm, vsl=vsl: e.tensor_tensor(
                out=vm[:], in0=vsl.unsqueeze(1).to_broadcast([16, 4, 128]),
                in1=hc[0:16, HC_CM4:HC_CM4 + 4].unsqueeze(2).to_broadcast([16, 4, 128]), op=ALU.mult),
                ["vts", "hc"], [K("vm")])
            S.add("pe", lambda e, khT=khT, vm=vm: e.matmul(pA[:], lhsT=khT[:], rhs=vm[:], start=True, stop=True),
                  [K("khT"), K("vm")], ["pA"])
            S.add("act", lambda e, S0b=S0b, S0=S0: e.activation(out=S0b[:], in_=S0[:], func=AF.Copy),
                  [K("S0")], [K("S0b")])
            S.add("pe", lambda e, kt_=kt_, qt=qt: e.matmul(pB[0:16, 0, 0:16], lhsT=kt_[:], rhs=qt[:], start=True, stop=True),
                  [K("kt"), K("qt")], ["pB0"])
            S.add("dve", lambda e, AT=AT: e.tensor_tensor(out=AT[:], in0=pB[0:16, 0, 0:16], in1=hc[0:16, HC_M4:HC_M4 + 16],
                                                          op=ALU.mult), ["pB0", "hc"], [K("AT")])
            S.add("pe", lambda e, vsl=vsl, AT=AT: e.matmul(pB[:, 1, 0:16], lhsT=vsl, rhs=AT[:], start=True, stop=True),
                  ["vts", K("AT")], ["pB1"])
            for i in range(4):
                S.add("pe", lambda e, i=i, S0b=S0b, qt=qt: e.matmul(pB[:, 2, i * 4:(i + 1) * 4], lhsT=S0b[:, i, :],
                                                                    rhs=qt[:, i * 4:(i + 1) * 4], start=True, stop=True),
                      [K("S0b"), K("qt")], ["pB2"])
            S.add("dve", lambda e, S0=S0, dl=dl: e.tensor_tensor(
                out=S0[:], in0=S0[:], in1=dl[:].unsqueeze(2).to_broadcast([128, 4, 128]), op=ALU.mult),
                [K("S0"), K("dl"), K("S0b")], [K("S0")])
            S.add("dve", lambda e, S0=S0: e.tensor_tensor(out=S0[:], in0=S0[:], in1=pA[:], op=ALU.add),
                  [K("S0"), "pA"], [K("S0")])
            dma("sp", st_smp[:, h, :, :].rearrange("i k v -> k i v"), S0[:], [K("S0")], [], "hst")
            evac(osb[:], pB[:, 1, 0:16], ["pB1"], [K("osb")], eng="act")
            S.add("dve", lambda e, osb=osb: e.tensor_tensor(out=osb[:], in0=osb[:], in1=pB[:, 2, 0:16], op=ALU.add),
                  [K("osb"), "pB2"], [K("osb")])
            S.add("act", lambda e, osq=osq, osb=osb: e.activation(out=osq[:], in_=osb[:], func=AF.Square),
                  [K("osb")], [K("osq")])
            S.add("pe", lambda e, osq=osq: e.matmul(pB[:, 3, 0:16], lhsT=onesf, rhs=osq[:], start=True, stop=True),
                  ["hc", K("osq")], ["pB3"])
            S.add("act", lambda e, rs=rs: e.activation(out=rs[:], in_=pB[:, 3, 0:16], func=AF.Sqrt, scale=1.0 / 128,
                                                       bias=EPS), ["pB3"], [K("rs")])
            S.add("dve", lambda e, rs=rs: e.reciprocal(out=rs[:], in_=rs[:]), [K("rs")], [K("rs")])
            S.add("dve", lambda e, osb=osb, rs=rs: e.tensor_tensor(out=osb[:], in0=osb[:], in1=rs[:], op=ALU.mult),
                  [K("osb"), K("rs")], [K("osb")])
            S.add("dve", lambda e, recs=recs, osb=osb, gst=gst, h=h: e.scalar_tensor_tensor(
                out=recs[:], in0=osb[:], scalar=og[:, h:h + 1], in1=gst[:], op0=ALU.mult, op1=ALU.mult),
                [K("osb"), "og", K("gst")], [K("recs")])
            dma("sp", mixT_s[4 + h, :, NOWN:NOWN + 16], recs[:], [K("recs")], ["mixT_s"], "hrec")
        S.barrier()
        ph.close()

    def attn_sample():
        ph = ExitStack()

        def sbp(name, shape, dt=F32):
            return ph.enter_context(nc.sbuf_tensor("as_" + name, list(shape), dt))

        def psp(name, shape, dt=F32):
            return ph.enter_context(nc.psum_tensor("pas_" + name, list(shape), dt))

        KTs = sbp("KTs", [128, 4, 8196], BF16)
        Vs = sbp("Vs", [128, 64, 512], BF16)
        ohr = sbp("ohr", [33, OHR_N], BF16)
        smc = sbp("smc", [128, 513])
        relT = sbp("relT", [33, 8]); relTx = sbp("relTx", [33, 4, 32], BF16)
        ptb = sbp("ptb", [128, 256], I32); idx = sbp("idx", [128, 256], I32)
        kpg = [sbp("kpg%d" % i, [128, 512]) for i in range(4)]
        vpg = [sbp("vpg%d" % i, [128, 512]) for i in range(4)]
        qz = sbp("qz", [128, 4, 32], BF16)
        ks = sbp("ks", [128, 4, 32]); ksb = sbp("ksb", [128, 4, 32], BF16)
        gsb = sbp("gsb", [32, 32]); top8 = sbp("top8", [32, 8]); mb = sbp("mb", [32, 32])
        Pb = [sbp("Pb%d" % i, [32, 512], BF16) for i in range(2)]
        PTs = [sbp("PTs%d" % i, [128, 4, 32], BF16) for i in range(2)]
        den = sbp("den", [32, 34]); dsum = sbp("dsum", [32, 1])
        Vnew = sbp("Vnew", [4, 512], BF16)
        acc = sbp("acc", [32, 512]); o32 = sbp("o32", [32, 64]); oT = sbp("oT", [64, 32], BF16)
        ptr = [psp("ptr%d" % i, [128, 4, 128]) for i in range(2)]
        pls = [psp("pls%d" % i, [32, 512]) for i in range(2)]
        ppt = psp("ppt", [128, 4, 32], BF16)
        pacc = psp("pacc", [32, 512])
        pgt = psp("pgt", [64, 32])

        dma("sp", ohr[:], ohr_in, [], ["ohr"], "sc0")
        dma("sp", smc[:], smc_in, [], ["smc"], "sc2")
        dma("sp", ptb[:], ptab_in.partition_broadcast(128), [], ["ptb"], "sc3")
        dma("sp", relT[0:32, :], relb_in, [], ["relT"], "sc1")
        S.add("dve", lambda e: e.memset(relT[32:33, :], NEGB), [], ["relT32"])
        S.add("dve", lambda e: e.memset(relTx[:], 0.0), [], ["relTx"])
        for t in range(4):
            S.add("dve", lambda e, t=t: e.tensor_copy(
                out=relTx[:, t, :].rearrange("p (h t) -> p h t", t=4)[:, :, t], in_=relT[:, :]),
                ["relT", "relT32", "relTx"], ["relTx"])
        S.add("dve", lambda e: e.tensor_scalar(out=idx[:], in0=ptb[:], scalar1=128.0, scalar2=smc[:, 0:1],
                                               op0=ALU.mult, op1=ALU.add), ["ptb", "smc"], ["idx"])
        S.add("dve", lambda e: e.memset(den[:], 0.0), [], ["den"])
        S.barrier()
        pc = 0
        for i in range(4):
            if i > 0:
                S.barrier()
            for j in range(64):
                sl = pc % 4
                pb = pc % 2
                pc += 1
                col = i * 64 + j
                S.add("pool", lambda e, sl=sl, col=col: e.indirect_dma_start(
                    out=kpg[sl][:, :], out_offset=None, in_=cache_k[:, :],
                    in_offset=bass.IndirectOffsetOnAxis(ap=idx[:, col:col + 1], axis=0)),
                    ["idx"], [("kpg", sl)], dma="sk%d" % sl)
                S.add("pool", lambda e, sl=sl, col=col: e.indirect_dma_start(
                    out=vpg[sl][:, :], out_offset=None, in_=cache_v[:, :],
                    in_offset=bass.IndirectOffsetOnAxis(ap=idx[:, col:col + 1], axis=0)),
                    ["idx"], [("vpg", sl)], dma="sv%d" % sl)
                for c in range(4):
                    S.add("pe", lambda e, sl=sl, c=c, pb=pb: e.transpose(
                        out=ptr[pb][:, c, :], in_=kpg[sl][:, c * 128:(c + 1) * 128], identity=ident[:]),
                        [("kpg", sl), "ident"], [("ptr", pb)])
                evac(KTs[:, :, j * 128:(j + 1) * 128], ptr[pb][:], [("ptr", pb)], ["KTs"])
                evac(Vs[:, j, :], vpg[sl][:], [("vpg", sl)], ["Vs"])
            dma("sp", KTs[:, :, 8192:8196],
                kT_s.rearrange("(c hl) d t -> (hl d) c t", hl=2)[:, :, 4096 + 4 * i:4096 + 4 * i + 4],
                [], ["KTs"], "sk9")
            dma("sp", Vnew[:], v_s[4096 + 4 * i:4096 + 4 * i + 4, :], [], ["Vnew"], "sv9")
            S.add("dve", lambda e: e.memset(qz[:], 0.0), [], ["qz"])
            for c in range(4):
                for hl in range(2):
                    dma("sp", qz[hl * 64:(hl + 1) * 64, c, c * 8 + hl * 4:c * 8 + hl * 4 + 4],
                        qT_s[2 * c + hl, 0:64, NOWN + 4 * i:NOWN + 4 * i + 4], ["qz"], ["qzd"], "sq")
            S.add("dve", lambda e: e.tensor_reduce(
                out=ks[:], in_=KTs[:, :, 0:8192].rearrange("p c (b t) -> p c b t", t=256), axis=AX.X, op=ALU.add),
                ["KTs"], ["ks"])
            S.add("act", lambda e: e.activation(out=ksb[:], in_=ks[:], func=AF.Copy), ["ks"], ["ksb"])
            for c in range(4):
                S.add("pe", lambda e, c=c: e.matmul(pgt[0:32, :], lhsT=qz[:, c, :], rhs=ksb[:, c, :],
                                                    start=(c == 0), stop=(c == 3)), ["qz", "qzd", "ksb"], ["pgt"])
            evac(gsb[:], pgt[0:32, :], ["pgt"], ["gsb"], eng="act")
            S.add("dve", lambda e: e.max(out=top8[:], in_=gsb[:]), ["gsb"], ["top8"])
            S.add("dve", lambda e: e.tensor_scalar(out=mb[:], in0=gsb[:], scalar1=top8[:, 2:3], scalar2=None,
                                                   op0=ALU.is_ge), ["gsb", "top8"], ["mb"])
            S.add("dve", lambda e: e.tensor_scalar(out=mb[:], in0=mb[:], scalar1=-1.0, scalar2=-NEGB,
                                                   op0=ALU.add, op1=ALU.mult), ["mb"], ["mb"])
            for g in range(17):
                nk = 512 if g < 16 else 4
                b2 = g % 2
                k0 = g * 512
                for c in range(4):
                    S.add("pe", lambda e, c=c, k0=k0, nk=nk, b2=b2: e.matmul(
                        pls[b2][:, 0:nk], lhsT=qz[:, c, :], rhs=KTs[:, c, k0:k0 + nk], start=(c == 0), stop=False),
                        ["qz", "qzd", "KTs"], [("pls", b2)])
                for t in range(4):
                    S.add("pe", lambda e, t=t, k0=k0, nk=nk, b2=b2: e.matmul(
                        pls[b2][:, 0:nk], lhsT=relTx[:, t, :], rhs=ohr[:, k0 + 3 - t:k0 + 3 - t + nk],
                        start=False, stop=(t == 3)), ["relTx", "ohr"], [("pls", b2)])
                if g < 16:
                    for hf in range(2):
                        blk = 2 * g + hf
                        S.add("act", lambda e, hf=hf, blk=blk, b2=b2: e.activation(
                            out=Pb[b2][:, hf * 256:(hf + 1) * 256], in_=pls[b2][:, hf * 256:(hf + 1) * 256],
                            func=AF.Exp, bias=mb[:, blk:blk + 1], accum_out=den[:, blk:blk + 1]),
                            [("pls", b2), "mb"], [("Pb", b2), "den"])
                    for kq in range(4):
                        S.add("pe", lambda e, kq=kq, b2=b2: e.transpose(
                            out=ppt[:, kq, :], in_=Pb[b2][:, kq * 128:(kq + 1) * 128], identity=identb[0:32, 0:32]),
                            [("Pb", b2), "identb"], ["ppt"])
                    evac(PTs[b2][:], ppt[:], ["ppt"], [("PTs", b2)])
                    for kq in range(4):
                        S.add("pe", lambda e, kq=kq, g=g, b2=b2: e.matmul(
                            pacc[:], lhsT=PTs[b2][:, kq, :], rhs=Vs[:, g * 4 + kq, :],
                            start=(g == 0 and kq == 0), stop=False), [("PTs", b2), "Vs"], ["pacc"])
                else:
                    S.add("act", lambda e, b2=b2: e.activation(
                        out=Pb[b2][:, 0:4], in_=pls[b2][:, 0:4], func=AF.Exp, accum_out=den[:, 32:33]),
                        [("pls", b2)], [("Pb", b2), "den"])
                    S.add("pe", lambda e, b2=b2: e.transpose(out=ppt[0:4, 0, :], in_=Pb[b2][:, 0:4],
                                                             identity=identb[0:32, 0:32]),
                          [("Pb", b2), "identb"], ["ppt"])
                    evac(PTs[b2][0:4, 0, :], ppt[0:4, 0, :], ["ppt"], [("PTs", b2)])
                    S.add("pe", lambda e, b2=b2: e.matmul(pacc[:], lhsT=PTs[b2][0:4, 0, :], rhs=Vnew[:, :],
                                                          start=False, stop=True), [("PTs", b2), "Vnew"], ["pacc"])
            evac(acc[:], pacc[:], ["pacc"], ["acc"], eng="act")
            S.add("dve", lambda e: e.tensor_tensor(out=acc[:], in0=acc[:], in1=smc[0:32, 1:513], op=ALU.mult),
                  ["acc", "smc"], ["acc"])
            S.add("dve", lambda e: e.tensor_reduce(out=o32[:], in_=acc[:].rearrange("p (h d) -> p d h", h=8),
                                                   axis=AX.X, op=ALU.add), ["acc"], ["o32"])
            S.add("dve", lambda e: e.tensor_reduce(out=dsum[:], in_=den[:, 0:33], axis=AX.X, op=ALU.add),
                  ["den"], ["dsum"])
            S.add("dve", lambda e: e.reciprocal(out=dsum[:], in_=dsum[:]), ["dsum"], ["dsum"])
            S.add("dve", lambda e: e.tensor_scalar(out=o32[:], in0=o32[:], scalar1=dsum[:, 0:1], scalar2=None,
                                                   op0=ALU.mult), ["o32", "dsum"], ["o32"])
            S.add("pe", lambda e: e.transpose(out=pgt[:, :], in_=o32[:, :], identity=ident[0:32, 0:32]),
                  ["o32", "ident"], ["pgt"])
            evac(oT[:], pgt[:, :], ["pgt"], ["oT"])
            for c in range(4):
                for hl in range(2):
                    dma("sp", mixT_s[c, hl * 64:(hl + 1) * 64, NOWN + 4 * i:NOWN + 4 * i + 4],
                        oT[:, c * 8 + hl * 4:c * 8 + hl * 4 + 4], ["oT"], ["mixT_s"], "so")
        S.barrier()
        ph.close()

    tiles = []
    for i in range(4):
        tiles.append(("pre", i * 512, 4, i * 512, None))
    for i in range(4):
        tiles.append(("own", NPRE + i * 512, 4, NPRE + i * 512, i * 512))
    tiles.append(("smp", R_S0, 1, 4096, NOWN))
    if stop_after == "a1":
        tiles = [tiles[4]]
    ph = ExitStack()
    EA = stage_env(ph)
    for ti, t in enumerate(tiles):
        EA.stage_a(*t, last=(ti == len(tiles) - 1))
    S.barrier()
    ph.close()
    if stop_after not in ("a1", "a"):
        hgrn_phase()
    if stop_after not in ("a1", "a", "h"):
        tvec_setup()
        attn_phase()
    if stop_after not in ("a1", "a", "h", "t", "b_nosmp"):
        hgrn_sample()
        if with_cache:
            attn_sample()
    if stop_after not in ("a1", "a", "h", "t"):
        tb = [("own", 4, i * 512) for i in range(4)]
        if stop_after != "b_nosmp":
            tb.append(("smp", 1, NOWN))
        stage_b_all(tb)

    S.emit(nc, es)
    es.close()
    return nc


_CACHE = {}


def _rel_bucket_np(dist):
    n = np.maximum(dist, 0).astype(np.int64)
    nf = np.maximum(n, 1).astype(np.float32)
    lg = np.log(nf / np.float32(16.0)).astype(np.float32)
    large = 16 + ((lg / np.float32(np.log(256.0))) * np.float32(16.0)).astype(np.float32).astype(np.int32)
    large = np.minimum(large, 31)
    return np.where(n < 16, n, large).astype(np.int64)


def _static_consts():
    hc = np.zeros((128, HC_N), np.float32)
    t = np.arange(512)
    hc[:, HC_RM16:HC_RM16 + 512] = (t % 16 != 0).astype(np.float32)[None, :]
    hc[:, HC_RM128:HC_RM128 + 512] = (t % 128 != 0).astype(np.float32)[None, :]
    si = np.arange(128)[:, None]
    ti = np.arange(128)[None, :]
    hc[:, HC_M16:HC_M16 + 128] = ((si // 16 == ti // 16) & (si <= ti)).astype(np.float32)
    hc[:, HC_CM16:HC_CM16 + 8] = (si // 16 == np.arange(8)[None, :]).astype(np.float32)
    t4 = np.arange(16)
    hc[:, HC_RM4:HC_RM4 + 16] = (t4 % 4 != 0).astype(np.float32)[None, :]
    s4 = np.arange(16)[:, None]
    hc[0:16, HC_M4:HC_M4 + 16] = ((s4 // 4 == t4[None, :] // 4) & (s4 <= t4[None, :])).astype(np.float32)
    hc[0:16, HC_CM4:HC_CM4 + 4] = (s4 // 4 == np.arange(4)[None, :]).astype(np.float32)
    hc[:, HC_ONES:HC_ONES + 128] = 1.0
    hc[:, HC_J:HC_J + 128] = (si == 127 - ti).astype(np.float32)
    blkoh = (np.arange(4096)[None, :] // 256 == np.arange(16)[:, None]).astype(np.float32).astype(ml_dtypes.bfloat16)
    oht = np.zeros((33, TV_N), np.float32)
    i = np.arange(TV_N)
    oht[32, i < TV_NEG] = 1.0
    bk = _rel_bucket_np(i - TV_NEG)
    pos = i >= TV_NEG
    oht[bk[pos], i[pos]] = 1.0
    ohr = np.zeros((33, OHR_N), np.float32)
    m = np.arange(OHR_N)
    dist = 8195 - m
    posd = dist >= 0
    ohr[32, ~posd] = 1.0
    ohr[_rel_bucket_np(dist[posd]), m[posd]] = 1.0
    smc = np.zeros((128, 513), np.float32)
    smc[:, 0] = np.arange(128)
    rows = np.arange(32)[:, None]
    cols = np.arange(512)[None, :]
    smc[0:32, 1:513] = (cols // 64 == rows // 4).astype(np.float32)
    return hc, blkoh, oht, ohr.astype(ml_dtypes.bfloat16), smc


def _consts(half):
    ident = np.eye(128, dtype=np.float32)
    gm = np.zeros((NOWN, 3, 16), np.float32)
    q = np.arange(NOWN)
    ownblk = 8 + q // 256
    for blk in range(16):
        past = blk < ownblk
        if half == 0:
            past = past & (blk >= 8)
        gm[:, 0, blk] = np.where(past, 0.0, -1e30)
        gm[:, 1, blk] = np.where(past, 1.0, 0.0)
        gm[:, 2, blk] = np.where(past | (blk == ownblk), 0.0, NEGB)
    return ident, gm


def _in_maps(inp, with_cache=True):
    maps = []
    if with_cache:
        ck = np.asarray(inp["cache_k"], dtype=np.float32).reshape(2560 * 128, 512)
        cv = np.asarray(inp["cache_v"], dtype=np.float32).reshape(2560 * 128, 512)
    f = lambda a: np.ascontiguousarray(np.asarray(a, dtype=np.float32))
    xp = f(inp["x_prompt"]); xs = f(inp["x_sample"])
    gains = np.stack([f(inp["ffn1_norm"])[0], f(inp["mix_norm"])[0], f(inp["ffn2_norm"])[0]])
    qkg = np.stack([f(inp["q_norm"])[0], f(inp["k_norm"])[0]])
    hc, blkoh, oht, ohr, smc = _static_consts()
    shared = {
        "ffn1_gate": f(inp["ffn1_gate"])[0], "ffn1_up": f(inp["ffn1_up"])[0], "ffn1_down": f(inp["ffn1_down"])[0],
        "ffn2_gate": f(inp["ffn2_gate"])[0], "ffn2_up": f(inp["ffn2_up"])[0], "ffn2_down": f(inp["ffn2_down"])[0],
        "w_in": f(inp["w_in"])[0], "w_out": f(inp["w_out"])[0], "gains": gains, "qkg": qkg,
        "hconst": hc, "blkoh": blkoh, "oht": oht, "relb": f(inp["rel_bias_table"]),
        "ogain": np.ascontiguousarray(f(inp["hgrn_out_norm"])[0].reshape(4, 128).T),
        "lb_logits": np.ascontiguousarray(f(inp["lb_logits"]).reshape(2, 4, 128).transpose(2, 0, 1)),
    }
    for c in range(8):
        b, half = c // 2, c % 2
        xin = np.zeros((NROWS, D), np.float32)
        if half == 1:
            xin[0:NPRE] = xp[b, 0:NPRE]
        xin[NPRE:NPRE + NOWN] = xp[b, half * 2048:(half + 1) * 2048]
        xin[R_S0:R_S0 + NSAMP] = xs[4 * c:4 * c + 4].reshape(NSAMP, D)
        ident, gm = _consts(half)
        m = dict(shared)
        m.update({"xin": xin, "ident": ident, "gmask": gm,
                  "st_in": np.ascontiguousarray(f(inp["state_hgrn"])[0, 4 * c:4 * c + 4])})
        if with_cache:
            m.update({"cache_k": ck, "cache_v": cv, "ohr": ohr, "smc": smc,
                      "ptab": np.ascontiguousarray(np.asarray(inp["page_table"], dtype=np.int32)[4 * c:4 * c + 4].reshape(1, 256))})
        maps.append(m)
    return maps


def kernel(**inp):
    if "nc" not in _CACHE:
        _CACHE["nc"] = build_program()
    nc = _CACHE["nc"]
    maps = _in_maps(inp)
    res = run_bass_kernel_spmd(nc, maps, core_ids=list(range(8)))
    R = res.results
    yp = np.zeros((4, 4096, D), np.float32); kp = np.zeros((1, 4, 4096, 8, 64), np.float32)
    vp = np.zeros((1, 4, 4096, 8, 64), np.float32)
    ys = np.zeros((32, 4, D), np.float32); ks = np.zeros((1, 32, 4, 8, 64), np.float32)
    vs = np.zeros((1, 32, 4, 8, 64), np.float32)
    for c in range(8):
        b, half = c // 2, c % 2
        sl = slice(half * 2048, (half + 1) * 2048)
        yp[b, sl] = R[c]["y_own"]
        kp[0, b, sl] = R[c]["k_own"].reshape(2048, 8, 64)
        vp[0, b, sl] = R[c]["v_own"].reshape(2048, 8, 64)
        ys[4 * c:4 * c + 4] = R[c]["y_smp"].reshape(4, 4, D)
        ks[0, 4 * c:4 * c + 4] = R[c]["k_smp"].reshape(4, 4, 8, 64)
        vs[0, 4 * c:4 * c + 4] = R[c]["v_smp"].reshape(4, 4, 8, 64)
    sp = np.zeros((1, 4, 4, 128, 128), np.float32)
    ssm = np.zeros((1, 32, 4, 128, 128), np.float32)
    for c in range(8):
        if c % 2 == 1:
            sp[0, c // 2] = R[c]["st_own"]
        ssm[0, 4 * c:4 * c + 4] = R[c]["st_smp"]
    return (yp, ys, kp, vp, sp, ks, vs, ssm)
```
